# Optimizing a Trainium2 kernel written in Bass

```python
import math
import jax, jax.numpy as jnp
from jax import lax
import numpy as np

D_MODEL = 1024
BATCH = 16
SEQ = 2048
DEPTH = 1

MIX_WIDTH = D_MODEL
SSM_WIDTH = MIX_WIDTH // 2
CONV_WIDTH = MIX_WIDTH - SSM_WIDTH
SSM_GROUP = 16
SSM_N_GROUPS = SSM_WIDTH // SSM_GROUP
SSM_STATE = 64
CONV_HEAD_DIM = 64
CONV_N_HEADS = CONV_WIDTH // CONV_HEAD_DIM
CONV_K = 3
DT_MIN = 0.001
DT_MAX = 0.1
EPS = 1e-6
IN_COLS = 2 * SSM_WIDTH + 4 * CONV_WIDTH

kernel_name = "hybrid_s5_shortconv_parallel_heads"


def rmsnorm(x, g):
    x32 = x.astype(jnp.float32)
    y = x32 * lax.rsqrt(jnp.mean(x32 * x32, axis=-1, keepdims=True) + EPS)
    return (y * g.astype(jnp.float32)).astype(x.dtype)


def _scan_combine(c1, c2):
    a1r, a1i, b1r, b1i = c1
    a2r, a2i, b2r, b2i = c2
    ar = a2r * a1r - a2i * a1i
    ai = a2r * a1i + a2i * a1r
    br = a2r * b1r - a2i * b1i + b2r
    bi = a2r * b1i + a2i * b1r + b2i
    return (ar, ai, br, bi)


def s5_mixer(u, a_re, a_im, log_dt, b_re, b_im, c_re, c_im, d_skip, w_glu, b_glu):
    bsz, seq, _ = u.shape
    f32 = jnp.float32
    u32 = u.astype(f32).reshape(bsz, seq, SSM_N_GROUPS, SSM_GROUP)
    a_re = a_re.astype(f32); a_im = a_im.astype(f32)
    dt = jnp.exp(log_dt.astype(f32))[:, None]
    mag = jnp.exp(a_re * dt)
    ab_re = mag * jnp.cos(a_im * dt)
    ab_im = mag * jnp.sin(a_im * dt)
    den = a_re * a_re + a_im * a_im
    p_re = ab_re - 1.0
    p_im = ab_im
    q_re = (p_re * a_re + p_im * a_im) / den
    q_im = (p_im * a_re - p_re * a_im) / den
    b_re = b_re.astype(f32); b_im = b_im.astype(f32)
    bb_re = q_re[..., None] * b_re - q_im[..., None] * b_im
    bb_im = q_re[..., None] * b_im + q_im[..., None] * b_re
    bu_re = jnp.einsum('blgh,gph->blgp', u32, bb_re)
    bu_im = jnp.einsum('blgh,gph->blgp', u32, bb_im)
    a_seq_re = jnp.broadcast_to(ab_re[None, None], (1, seq, SSM_N_GROUPS, SSM_STATE))
    a_seq_im = jnp.broadcast_to(ab_im[None, None], (1, seq, SSM_N_GROUPS, SSM_STATE))
    _, _, s_re, s_im = lax.associative_scan(
        _scan_combine, (a_seq_re, a_seq_im, bu_re, bu_im), axis=1)
    y = (jnp.einsum('blgp,ghp->blgh', s_re, c_re.astype(f32))
         - jnp.einsum('blgp,ghp->blgh', s_im, c_im.astype(f32))
         + d_skip.astype(f32) * u32)
    y = y.reshape(bsz, seq, SSM_WIDTH)
    y = jax.nn.gelu(y)
    y = y * jax.nn.sigmoid(jnp.einsum('bld,de->ble', y, w_glu.astype(f32)) + b_glu.astype(f32))
    return y.astype(u.dtype)


def short_conv_mixer(h, gate_b, gate_c, conv_w):
    v = gate_c * h
    vp = jnp.pad(v, ((0, 0), (CONV_K - 1, 0), (0, 0)))
    seq = h.shape[1]
    y = sum(conv_w[k] * vp[:, k:k + seq] for k in range(CONV_K))
    return gate_b * y


def setup_inputs(seed: int = 0) -> dict:
    key = jax.random.key(seed)
    ks = jax.random.split(key, 20)
    f32 = jnp.float32
    x = jax.random.normal(ks[0], (BATCH, SEQ, D_MODEL), f32)
    norm_gain = 1.0 + 0.02 * jax.random.normal(ks[1], (DEPTH, D_MODEL), f32)
    w_in = jax.random.normal(ks[2], (DEPTH, D_MODEL, IN_COLS), f32) * D_MODEL ** -0.5
    n = jnp.arange(SSM_STATE, dtype=f32)
    ssm_a_re = -0.5 + 0.01 * jax.random.normal(ks[3], (DEPTH, SSM_N_GROUPS, SSM_STATE), f32)
    ssm_a_im = math.pi * n + 0.01 * jax.random.normal(ks[4], (DEPTH, SSM_N_GROUPS, SSM_STATE), f32)
    ssm_log_dt = jax.random.uniform(ks[5], (DEPTH, SSM_N_GROUPS), f32,
                                    math.log(DT_MIN), math.log(DT_MAX))
    bscale = (2.0 * SSM_GROUP) ** -0.5
    ssm_b_re = jax.random.normal(ks[6], (DEPTH, SSM_N_GROUPS, SSM_STATE, SSM_GROUP), f32) * bscale
    ssm_b_im = jax.random.normal(ks[7], (DEPTH, SSM_N_GROUPS, SSM_STATE, SSM_GROUP), f32) * bscale
    cscale = (2.0 * SSM_STATE) ** -0.5
    ssm_c_re = jax.random.normal(ks[8], (DEPTH, SSM_N_GROUPS, SSM_GROUP, SSM_STATE), f32) * cscale
    ssm_c_im = jax.random.normal(ks[9], (DEPTH, SSM_N_GROUPS, SSM_GROUP, SSM_STATE), f32) * cscale
    ssm_d = 1.0 + 0.1 * jax.random.normal(ks[10], (DEPTH, SSM_N_GROUPS, SSM_GROUP), f32)
    w_glu = jax.random.normal(ks[11], (DEPTH, SSM_WIDTH, SSM_WIDTH), f32) * SSM_WIDTH ** -0.5
    b_glu = 0.01 * jax.random.normal(ks[12], (DEPTH, SSM_WIDTH), f32)
    conv_w = jax.random.normal(ks[13], (DEPTH, CONV_K, CONV_WIDTH), f32) * CONV_K ** -0.5
    w_out = jax.random.normal(ks[14], (DEPTH, MIX_WIDTH, D_MODEL), f32) * MIX_WIDTH ** -0.5
    final_norm_gain = 1.0 + 0.02 * jax.random.normal(ks[15], (D_MODEL,), f32)
    return {"x": x, "norm_gain": norm_gain, "w_in": w_in,
            "ssm_a_re": ssm_a_re, "ssm_a_im": ssm_a_im, "ssm_log_dt": ssm_log_dt,
            "ssm_b_re": ssm_b_re, "ssm_b_im": ssm_b_im,
            "ssm_c_re": ssm_c_re, "ssm_c_im": ssm_c_im, "ssm_d": ssm_d,
            "w_glu": w_glu, "b_glu": b_glu, "conv_w": conv_w, "w_out": w_out,
            "final_norm_gain": final_norm_gain}


def reference(x, norm_gain, w_in, ssm_a_re, ssm_a_im, ssm_log_dt, ssm_b_re, ssm_b_im,
              ssm_c_re, ssm_c_im, ssm_d, w_glu, b_glu, conv_w, w_out, final_norm_gain):
    h = x
    for l in range(DEPTH):
        xn = rmsnorm(h, norm_gain[l])
        proj = jnp.einsum('bld,dc->blc', xn, w_in[l])
        s0 = SSM_WIDTH
        u_ssm = proj[..., :s0]
        z_ssm = proj[..., s0:2 * s0]
        c0 = 2 * s0
        h_conv = proj[..., c0:c0 + CONV_WIDTH]
        b_conv = proj[..., c0 + CONV_WIDTH:c0 + 2 * CONV_WIDTH]
        c_conv = proj[..., c0 + 2 * CONV_WIDTH:c0 + 3 * CONV_WIDTH]
        z_conv = proj[..., c0 + 3 * CONV_WIDTH:c0 + 4 * CONV_WIDTH]
        y_ssm = s5_mixer(u_ssm, ssm_a_re[l], ssm_a_im[l], ssm_log_dt[l],
                         ssm_b_re[l], ssm_b_im[l], ssm_c_re[l], ssm_c_im[l],
                         ssm_d[l], w_glu[l], b_glu[l]) * jax.nn.silu(z_ssm)
        y_conv = short_conv_mixer(h_conv, b_conv, c_conv, conv_w[l]) * jax.nn.silu(z_conv)
        y = jnp.concatenate([y_ssm, y_conv], axis=-1)
        h = h + jnp.einsum('blc,cd->bld', y, w_out[l])
    return rmsnorm(h, final_norm_gain)
```

```python
import math
from contextlib import ExitStack
import numpy as np
import concourse.bass as bass
import concourse.mybir as mybir
from concourse.bass_utils import run_bass_kernel_spmd

F32 = mybir.dt.float32
BF16 = mybir.dt.bfloat16
ALU = mybir.AluOpType
AF = mybir.ActivationFunctionType

NCORES = 8
NSEG = 8
EPS = 1e-6


class Sched:
    ENG = ("pe", "act", "dve", "pool", "sp")

    def __init__(self, nc, stack):
        self.nc = nc
        self.stack = stack
        self.prog = {e: [] for e in self.ENG}
        self.count = {e: 0 for e in self.ENG}
        self.waited = {e: {} for e in self.ENG}
        self.sems = {}
        self.semval = {}
        self.res_w = {}
        self.res_r = {}
        for e in self.ENG:
            self._sem("eng_" + e)

    def _sem(self, name):
        if name not in self.sems:
            self.sems[name] = self.stack.enter_context(self.nc.semaphore(name))
            self.semval[name] = 0
        return self.sems[name]

    def _wait(self, eng, tok):
        if tok is None:
            return
        name, val = tok
        if eng == "pe" and name == "eng_pe":
            return
        if self.waited[eng].get(name, 0) >= val:
            return
        self.waited[eng][name] = val
        self.prog[eng].append(("wait", name, val))

    def emit(self, eng, fn, reads=(), writes=(), dma=None):
        deps = []
        for r in reads:
            if r in self.res_w:
                deps.append(self.res_w[r])
        for w in writes:
            if w in self.res_w:
                deps.append(self.res_w[w])
            for n, v in self.res_r.get(w, {}).items():
                deps.append((n, v))
        agg = {}
        for n, v in deps:
            if agg.get(n, 0) < v:
                agg[n] = v
        for t in agg.items():
            self._wait(eng, t)
        if dma is None:
            name = "eng_" + eng
            self.count[eng] += 1
            tok = (name, self.count[eng])
            self.prog[eng].append(("op", fn, name, 1))
        else:
            self._sem(dma)
            self.semval[dma] += 16
            tok = (dma, self.semval[dma])
            self.prog[eng].append(("op", fn, dma, 16))
        for w in writes:
            self.res_w[w] = tok
            self.res_r[w] = {}
        for r in reads:
            d = self.res_r.setdefault(r, {})
            if d.get(tok[0], 0) < tok[1]:
                d[tok[0]] = tok[1]
        return tok

    def emit_group(self, eng, fns, reads=(), writes=(), dma=None):
        deps = []
        for r in reads:
            if r in self.res_w:
                deps.append(self.res_w[r])
        for w in writes:
            if w in self.res_w:
                deps.append(self.res_w[w])
            for n, v in self.res_r.get(w, {}).items():
                deps.append((n, v))
        agg = {}
        for n, v in deps:
            if agg.get(n, 0) < v:
                agg[n] = v
        for t in agg.items():
            self._wait(eng, t)
        self._sem(dma)
        for fn in fns:
            self.semval[dma] += 16
            self.prog[eng].append(("op", fn, dma, 16))
        tok = (dma, self.semval[dma])
        for w in writes:
            self.res_w[w] = tok
            self.res_r[w] = {}
        for r in reads:
            d = self.res_r.setdefault(r, {})
            if d.get(tok[0], 0) < tok[1]:
                d[tok[0]] = tok[1]
        return tok

    def barrier(self):
        toks = [("eng_" + e, self.count[e]) for e in self.ENG if self.count[e] > 0]
        toks += [(n, v) for n, v in self.semval.items() if not n.startswith("eng_") and v > 0]
        for e in self.ENG:
            for t in toks:
                self._wait(e, t)

    def finish(self, final_tokens):
        agg = {}
        for n, v in final_tokens:
            if agg.get(n, 0) < v:
                agg[n] = v
        for t in agg.items():
            self._wait("sp", t)
        nc = self.nc
        sems = self.sems
        prog = self.prog

        def replay(engname, e):
            for it in prog[engname]:
                if it[0] == "wait":
                    e.wait_ge(sems[it[1]], it[2])
                else:
                    it[1](e).then_inc(sems[it[2]], it[3])

        with nc.Block() as block:
            @block.tensor
            def _(e):
                replay("pe", e)

            @block.scalar
            def _(e):
                replay("act", e)

            @block.vector
            def _(e):
                replay("dve", e)

            @block.gpsimd
            def _(e):
                replay("pool", e)

            @block.sync
            def _(e):
                replay("sp", e)


def build_program(nseg=NSEG, dbg=False):
    nc = bass.Bass("TRN2", target_bir_lowering=False)

    def dram(name, shape, kind="ExternalInput"):
        return nc.dram_tensor(name, shape, F32, kind=kind).ap()

    x = dram("x", [4096, 1024])
    ng = dram("norm_gain", [1024])
    w_in = dram("w_in", [1024, 3072])
    a_re = dram("ssm_a_re", [32, 64])
    a_im = dram("ssm_a_im", [32, 64])
    log_dt = dram("ssm_log_dt", [32])
    b_re = dram("ssm_b_re", [32, 64, 16])
    b_im = dram("ssm_b_im", [32, 64, 16])
    c_re = dram("ssm_c_re", [32, 16, 64])
    c_im = dram("ssm_c_im", [32, 16, 64])
    d_sk = dram("ssm_d", [32, 16])
    w_glu = dram("w_glu", [512, 512])
    b_glu = dram("b_glu", [512])
    conv_w = dram("conv_w", [3, 512])
    w_out = dram("w_out", [1024, 1024])
    fng = dram("final_norm_gain", [1024])
    out = dram("out", [4096, 1024], kind="ExternalOutput")

    xv = x.rearrange("(g j s) d -> g s j d", g=8, j=64, s=8)
    ov = out.rearrange("(g j s) d -> g s j d", g=8, j=64, s=8)

    with ExitStack() as st:
        S = Sched(nc, st)

        tstack = [st]

        def T(name, shape, dt=F32):
            return tstack[0].enter_context(nc.sbuf_tensor(name, shape, dt))

        def P(name, shape=(128, 512), dt=F32):
            return st.enter_context(nc.psum_tensor(name, list(shape), dt))

        w_in_bf = T("w_in_bf", [128, 8, 3072], BF16)
        w_out_bf = T("w_out_bf", [128, 8, 1024], BF16)
        w_glu_bf = T("w_glu_bf", [128, 4, 512], BF16)
        BbigT = T("BbigT", [128, 32, 128], BF16)
        Kb0T = T("Kb0T", [128, 32, 128], BF16)
        BbigswT = T("BbigswT", [128, 32, 128], BF16)
        CbT = T("CbT", [128, 32, 128], F32)
        RHO0 = T("RHO0", [128, 32], F32)
        COSm = T("COSm", [128, 32, 64], F32)
        SINm = T("SINm", [128, 32, 64], F32)
        ident = T("ident", [128, 128], F32)
        identb = T("identb", [128, 128], BF16)
        PswT = T("PswT", [128, 128], F32)
        smallT = T("smallT", [128, 24], F32)
        gcol = smallT[:, 0:8]
        bglu = smallT[:, 8:12]
        fg = T("fg", [128, 1024], F32)
        epsT = T("epsT", [128, 1], F32)

        PPn = 6
        PP = [P(f"pp{j}") for j in range(PPn)]
        PTB = [P(f"ptb{j}", (128, 1024), BF16) for j in range(2)]
        pp_ctr = [0]
        pt_ctr = [0]

        def next_pt():
            j = pt_ctr[0] % 2
            pt_ctr[0] += 1
            return PTB[j], ("ptb", j)

        def next_pp():
            j = pp_ctr[0] % PPn
            pp_ctr[0] += 1
            return PP[j], ("pp", j)

        st_setup = ExitStack()
        tstack[0] = st_setup
        S.emit("pool", lambda e: e.memset(ident[:], 0.0), writes=["ident"])
        S.emit("pool", lambda e: e.affine_select(out=ident[:], in_=ident[:], compare_op=ALU.not_equal, fill=1.0,
                                                 base=0, pattern=[[-1, 128]], channel_multiplier=1),
               reads=["ident"], writes=["ident"])
        S.emit("dve", lambda e: e.tensor_copy(out=identb[:], in_=ident[:]), reads=["ident"], writes=["identb"])
        offd = T("offd", [128, 128], F32)
        S.emit("pool", lambda e: e.memset(offd[:], 0.0), writes=["offd"])
        S.emit("pool", lambda e: e.affine_select(out=offd[:], in_=offd[:], compare_op=ALU.not_equal, fill=1.0,
                                                 base=64, pattern=[[-1, 128]], channel_multiplier=1),
               reads=["offd"], writes=["offd"])
        S.emit("pool", lambda e: e.affine_select(out=offd[:], in_=offd[:], compare_op=ALU.not_equal, fill=1.0,
                                                 base=-64, pattern=[[-1, 128]], channel_multiplier=1),
               reads=["offd"], writes=["offd"])
        mask = T("mask", [128, 128], F32)
        S.emit("pool", lambda e: e.memset(mask[:], 1.0), writes=["mask"])
        S.emit("pool", lambda e: e.affine_select(out=mask[:].rearrange("p (r h) -> p r h", r=8, h=16),
                                                 in_=mask[:].rearrange("p (r h) -> p r h", r=8, h=16),
                                                 compare_op=ALU.is_ge, fill=0.0,
                                                 base=15, pattern=[[16, 8], [0, 16]], channel_multiplier=-1),
               reads=["mask"], writes=["mask"])
        SG = T("SG", [128, 1], F32)
        NSG = T("NSG", [128, 1], F32)
        S.emit("pool", lambda e: e.memset(SG[0:64, :], -1.0), writes=["SGa"])
        S.emit("pool", lambda e: e.memset(SG[64:128, :], 1.0), writes=["SGb"])
        S.emit("pool", lambda e: e.memset(NSG[0:64, :], 1.0), writes=["NSGa"])
        S.emit("pool", lambda e: e.memset(NSG[64:128, :], -1.0), writes=["NSGb"])
        SGk = ["SGa", "SGb"]
        NSGk = ["NSGa", "NSGb"]
        S.emit("pool", lambda e: e.memset(epsT[:], EPS), writes=["epsT"])
        S.emit("dve", lambda e: e.tensor_scalar(out=PswT[:], in0=offd[:], scalar1=NSG[:, 0:1], scalar2=None, op0=ALU.mult),
               reads=["offd"] + NSGk, writes=["PswT"])

        small24 = T("small24", [24, 128], F32)
        S.emit("sp", lambda e: e.dma_start(out=small24[0:8, :], in_=ng.rearrange("(k p) -> k p", p=128)), writes=[("small24", 0)], dma="ld_s0")
        S.emit("sp", lambda e: e.dma_start(out=small24[8:12, :], in_=b_glu.rearrange("(k p) -> k p", p=128)), writes=[("small24", 1)], dma="ld_s1")
        S.emit("sp", lambda e: e.dma_start(out=small24[12:24, :], in_=conv_w.rearrange("k (c p) -> (k c) p", p=128)), writes=[("small24", 2)], dma="ld_s2")
        pp, ppk = next_pp()
        S.emit("pe", lambda e, pp=pp: e.transpose(out=pp[:, 0:24], in_=small24[:, :], identity=ident[0:24, 0:24]),
               reads=[("small24", 0), ("small24", 1), ("small24", 2), "ident"], writes=[ppk])
        S.emit("dve", lambda e, pp=pp: e.tensor_copy(out=smallT[:], in_=pp[:, 0:24]), reads=[ppk], writes=["gcol", "bglu", "convw"])
        convwk = ["convw"]
        S.emit("sp", lambda e: e.dma_start(out=fg[:], in_=bass.AP(fng.tensor, 0, [[0, 128], [1, 1024]])),
               writes=["fg"], dma="ld_s3")

        NE = 17

        def eidx(ev):
            return 17 if ev == 32 else ev + 8

        a32 = T("a32", [32, 2, 128], F32)
        for j, src in enumerate((a_re, a_im)):
            for h in range(2):
                S.emit("sp", lambda e, j=j, h=h, src=src: e.dma_start(out=a32[:, j, 64 * h:64 * h + 64], in_=src),
                       writes=[("a32", j, h)], dma=f"ld_a32_{j}{h}")
        ldt = T("ldt", [128, 32], F32)
        S.emit("sp", lambda e: e.dma_start(out=ldt[:], in_=bass.AP(log_dt.tensor, 0, [[0, 128], [1, 32]])),
               writes=["ldt"], dma="ld_s5")
        BA = T("BA", [128, 32, 16], F32)
        BB = T("BB", [128, 32, 16], F32)
        brv = b_re.rearrange("g p h -> p g h")
        biv = b_im.rearrange("g p h -> p g h")
        S.emit("sp", lambda e: e.dma_start(out=BA[0:64], in_=brv), writes=["BA0"], dma="ld_s6")
        S.emit("sp", lambda e: e.dma_start(out=BA[64:128], in_=biv), writes=["BA1"], dma="ld_s7")
        S.emit("sp", lambda e: e.dma_start(out=BB[0:64], in_=biv), writes=["BB0"], dma="ld_s8")
        S.emit("sp", lambda e: e.dma_start(out=BB[64:128], in_=brv), writes=["BB1"], dma="ld_s9")
        BAk = ["BA0", "BA1"]
        BBk = ["BB0", "BB1"]
        Cin1 = T("Cin1", [128, 4, 128], F32)
        Cin2 = T("Cin2", [128, 4, 128], F32)
        crv = c_re.rearrange("(a g) h p -> (g h) a p", a=4)
        civ = c_im.rearrange("(a g) h p -> (g h) a p", a=4)
        S.emit("sp", lambda e: e.dma_start(out=Cin1[:, :, 0:64], in_=crv), writes=["Cin1a"], dma="ld_s10")
        S.emit("sp", lambda e: e.dma_start(out=Cin1[:, :, 64:128], in_=civ), writes=["Cin1b"], dma="ld_s11")
        S.emit("sp", lambda e: e.dma_start(out=Cin2[:, :, 0:64], in_=civ), writes=["Cin2a"], dma="ld_s12")
        S.emit("sp", lambda e: e.dma_start(out=Cin2[:, :, 64:128], in_=crv), writes=["Cin2b"], dma="ld_s13")
        dd = T("dd", [32, 8, 16], F32)
        S.emit("sp", lambda e: e.dma_start(out=dd[:], in_=bass.AP(d_sk.tensor, 0, [[16, 32], [0, 8], [1, 16]])),
               writes=["dd"], dma="ld_s14")

        aT = T("aT", [128, 2, 32], F32)
        dcol = T("dcol", [128, 32], F32)
        pp, ppk = next_pp()
        for j in range(2):
            S.emit("pe", lambda e, j=j, pp=pp: e.transpose(out=pp[:, 32 * j:32 * j + 32], in_=a32[:, j, :], identity=ident[0:32, 0:32]),
                   reads=[("a32", j, 0), ("a32", j, 1), "ident"], writes=[ppk])
        S.emit("pe", lambda e, pp=pp: e.transpose(out=pp[:, 64:96], in_=dd[:].rearrange("p a b -> p (a b)"), identity=ident[0:32, 0:32]),
               reads=["dd", "ident"], writes=[ppk])
        S.emit("dve", lambda e, pp=pp: e.tensor_copy(out=aT[:].rearrange("p a b -> p (a b)"), in_=pp[:, 0:64]), reads=[ppk], writes=["aT"])
        S.emit("dve", lambda e, pp=pp: e.tensor_copy(out=dcol[:], in_=pp[:, 64:96]), reads=[ppk], writes=["dcol"])
        V1 = T("V1", [128, 32, 16], F32)
        V2 = T("V2", [128, 32, 16], F32)
        for (Cin, Vt, ck, vk) in ((Cin1, V1, ["Cin1a", "Cin1b"], "V1"), (Cin2, V2, ["Cin2a", "Cin2b"], "V2")):
            pp, ppk = next_pp()
            for a in range(4):
                S.emit("pe", lambda e, a=a, pp=pp, Cin=Cin: e.transpose(out=pp[:, 128 * a:128 * a + 128], in_=Cin[:, a, :], identity=ident[:]),
                       reads=ck + ["ident"], writes=[ppk])
            S.emit("dve", lambda e, pp=pp, Vt=Vt: e.tensor_copy(out=Vt[:].rearrange("p g h -> p (g h)"), in_=pp[:]), reads=[ppk], writes=[vk])
        S.emit("dve", lambda e: e.tensor_scalar(out=V1[64:128], in0=V1[64:128], scalar1=-1.0, scalar2=None, op0=ALU.mult),
               reads=["V1"], writes=["V1"])

        dt = T("dt", [128, 32], F32)
        dtr = T("dtr", [128, 32], F32)
        dti = T("dti", [128, 32], F32)
        S.emit("act", lambda e: e.activation(out=dt[:], in_=ldt[:], func=AF.Exp), reads=["ldt"], writes=["dt"])
        S.emit("dve", lambda e: e.tensor_tensor(out=dtr[:], in0=dt[:], in1=aT[:, 0, :], op=ALU.mult), reads=["dt", "aT"], writes=["dtr"])
        S.emit("dve", lambda e: e.tensor_tensor(out=dti[:], in0=dt[:], in1=aT[:, 1, :], op=ALU.mult), reads=["dt", "aT"], writes=["dti"])
        EX = T("EX", [128, NE], F32)
        for ev in list(range(-8, 9)):
            S.emit("pool", lambda e, ev=ev: e.memset(EX[:, eidx(ev):eidx(ev) + 1], float(ev)), writes=[("EX", ev)])
        EXk = [("EX", ev) for ev in list(range(-8, 9))]
        MAG = T("MAG", [128, 32, NE], F32)
        SINt = T("SINt", [128, 32, NE], F32)
        COSt = T("COSt", [128, 32, NE], F32)
        ZR = T("ZR", [128, 32, NE], F32)
        ZI = T("ZI", [128, 32, NE], F32)
        exb = EX[:].unsqueeze(1).broadcast_to([128, 32, NE])
        S.emit("dve", lambda e: e.tensor_tensor(out=MAG[:], in0=dtr[:].unsqueeze(2).broadcast_to([128, 32, NE]), in1=exb, op=ALU.mult),
               reads=["dtr"] + EXk, writes=["MAG"])
        S.emit("act", lambda e: e.activation(out=MAG[:], in_=MAG[:], func=AF.Exp), reads=["MAG"], writes=["MAG"])

        def cmul_block(eng_r, eng_i, C, Sn, cname, sname, base, n, lvl, tmps, src0=None, dst0=None, midx=None, tkeys=None, wlvl=None):
            tr1, tr2, ti1, ti2 = tmps
            src0 = base if src0 is None else src0
            dst0 = base + n if dst0 is None else dst0
            midx = base + n - 1 if midx is None else midx
            wlvl = lvl + 1 if wlvl is None else wlvl
            k1, k2, k3, k4 = tkeys if tkeys is not None else (cname + "_t1", cname + "_t2", sname + "_t1", sname + "_t2")
            csrc, ssrc = C[:, :, src0:src0 + n], Sn[:, :, src0:src0 + n]
            cm = C[:, :, midx:midx + 1].broadcast_to([128, 32, n])
            sm = Sn[:, :, midx:midx + 1].broadcast_to([128, 32, n])
            rk = [(cname, l) for l in range(lvl + 1)] + [(sname, l) for l in range(lvl + 1)]
            S.emit(eng_r, lambda e: e.tensor_tensor(out=tr1[:, :, 0:n], in0=csrc, in1=cm, op=ALU.mult), reads=rk, writes=[k1])
            S.emit(eng_r, lambda e: e.tensor_tensor(out=tr2[:, :, 0:n], in0=ssrc, in1=sm, op=ALU.mult), reads=rk, writes=[k2])
            S.emit(eng_r, lambda e: e.tensor_tensor(out=C[:, :, dst0:dst0 + n], in0=tr1[:, :, 0:n], in1=tr2[:, :, 0:n], op=ALU.subtract),
                   reads=[k1, k2], writes=[(cname, wlvl)])
            S.emit(eng_i, lambda e: e.tensor_tensor(out=ti1[:, :, 0:n], in0=csrc, in1=sm, op=ALU.mult), reads=rk, writes=[k3])
            S.emit(eng_i, lambda e: e.tensor_tensor(out=ti2[:, :, 0:n], in0=ssrc, in1=cm, op=ALU.mult), reads=rk, writes=[k4])
            S.emit(eng_i, lambda e: e.tensor_tensor(out=Sn[:, :, dst0:dst0 + n], in0=ti1[:, :, 0:n], in1=ti2[:, :, 0:n], op=ALU.add),
                   reads=[k3, k4], writes=[(sname, wlvl)])

        halfpi = T("halfpi", [128, 1], F32)
        S.emit("pool", lambda e: e.memset(halfpi[:], 0.5 * math.pi), writes=["halfpi"])
        u8 = T("u8", [128, 32], F32)
        cu = T("cu", [128, 32], F32)
        su = T("su", [128, 32], F32)
        tq_a = T("tq_a", [128, 32], F32)
        tq_b = T("tq_b", [128, 32], F32)
        S.emit("dve", lambda e: e.tensor_scalar(out=u8[:], in0=dti[:], scalar1=0.125, scalar2=None, op0=ALU.mult), reads=["dti"], writes=["u8"])
        S.emit("act", lambda e: e.activation(out=su[:], in_=u8[:], func=AF.Sin), reads=["u8"], writes=["su"])
        S.emit("act", lambda e: e.activation(out=cu[:], in_=u8[:], func=AF.Sin, scale=-1.0, bias=halfpi[:, 0:1]), reads=["u8", "halfpi"], writes=["cu"])
        for _ in range(3):
            S.emit("dve", lambda e: e.tensor_tensor(out=tq_a[:], in0=cu[:], in1=cu[:], op=ALU.mult), reads=["cu"], writes=["tq_a"])
            S.emit("dve", lambda e: e.tensor_tensor(out=tq_b[:], in0=su[:], in1=su[:], op=ALU.mult), reads=["su"], writes=["tq_b"])
            S.emit("dve", lambda e: e.scalar_tensor_tensor(out=su[:], in0=cu[:], scalar=2.0, in1=su[:], op0=ALU.mult, op1=ALU.mult), reads=["cu", "su"], writes=["su"])
            S.emit("dve", lambda e: e.tensor_tensor(out=cu[:], in0=tq_a[:], in1=tq_b[:], op=ALU.subtract), reads=["tq_a", "tq_b"], writes=["cu"])
        S.emit("pool", lambda e: e.memset(COSt[:], 1.0), writes=["COSt_init"])
        S.emit("pool", lambda e: e.memset(SINt[:], 0.0), writes=["SINt_init"])
        S.emit("dve", lambda e: e.tensor_copy(out=COSt[:, :, 9], in_=cu[:]), reads=["cu", "COSt_init"], writes=[("COSt", 0)])
        S.emit("dve", lambda e: e.tensor_copy(out=SINt[:, :, 9], in_=su[:]), reads=["su", "SINt_init"], writes=[("SINt", 0)])
        cmt = [T(f"cmt{j}", [128, 32, 4], F32) for j in range(4)]
        for lvl, n in enumerate((1, 2, 4)):
            cmul_block("dve", "dve", COSt, SINt, "COSt", "SINt", 9, n, lvl, cmt)
        posk = [("COSt", l) for l in range(4)] + [("SINt", l) for l in range(4)]
        for ev in range(1, 9):
            S.emit("dve", lambda e, ev=ev: e.tensor_copy(out=COSt[:, :, 8 - ev], in_=COSt[:, :, 8 + ev]), reads=posk, writes=[("COStn", ev)])
            S.emit("dve", lambda e, ev=ev: e.tensor_scalar(out=SINt[:, :, 8 - ev], in0=SINt[:, :, 8 + ev], scalar1=-1.0, scalar2=None, op0=ALU.mult),
                   reads=posk, writes=[("SINtn", ev)])
        allc = posk + [("COStn", ev) for ev in range(1, 9)] + ["COSt_init"]
        alls = posk + [("SINtn", ev) for ev in range(1, 9)] + ["SINt_init"]
        S.emit("dve", lambda e: e.tensor_copy(out=tq_a[:], in_=tq_a[:]), reads=allc, writes=["COSt"])
        S.emit("dve", lambda e: e.tensor_copy(out=tq_b[:], in_=tq_b[:]), reads=alls, writes=["SINt"])
        S.emit("dve", lambda e: e.tensor_tensor(out=ZR[:], in0=MAG[:], in1=COSt[:], op=ALU.mult), reads=["MAG", "COSt"], writes=["ZR"])
        S.emit("dve", lambda e: e.tensor_tensor(out=ZI[:], in0=MAG[:], in1=SINt[:], op=ALU.mult), reads=["MAG", "SINt"], writes=["ZI"])

        t = [T(f"tq{j}", [128, 32], F32) for j in range(6)]
        qr = T("qr", [128, 32], F32)
        qi = T("qi", [128, 32], F32)
        are, aim = aT[:, 0, :], aT[:, 1, :]
        zr1, zi1 = ZR[:, :, eidx(1)], ZI[:, :, eidx(1)]
        S.emit("dve", lambda e: e.tensor_tensor(out=t[0][:], in0=are, in1=are, op=ALU.mult), reads=["aT"], writes=["tq0"])
        S.emit("dve", lambda e: e.tensor_tensor(out=t[1][:], in0=aim, in1=aim, op=ALU.mult), reads=["aT"], writes=["tq1"])
        S.emit("dve", lambda e: e.tensor_tensor(out=t[0][:], in0=t[0][:], in1=t[1][:], op=ALU.add), reads=["tq0", "tq1"], writes=["tq0"])
        S.emit("dve", lambda e: e.reciprocal(out=t[0][:], in_=t[0][:]), reads=["tq0"], writes=["tq0"])
        S.emit("dve", lambda e: e.tensor_scalar(out=t[2][:], in0=zr1, scalar1=-1.0, scalar2=None, op0=ALU.add), reads=["ZR"], writes=["tq2"])
        S.emit("dve", lambda e: e.tensor_tensor(out=t[3][:], in0=t[2][:], in1=are, op=ALU.mult), reads=["tq2", "aT"], writes=["tq3"])
        S.emit("dve", lambda e: e.tensor_tensor(out=t[4][:], in0=zi1, in1=aim, op=ALU.mult), reads=["ZI", "aT"], writes=["tq4"])
        S.emit("dve", lambda e: e.tensor_tensor(out=t[3][:], in0=t[3][:], in1=t[4][:], op=ALU.add), reads=["tq3", "tq4"], writes=["tq3"])
        S.emit("dve", lambda e: e.tensor_tensor(out=qr[:], in0=t[3][:], in1=t[0][:], op=ALU.mult), reads=["tq3", "tq0"], writes=["qr"])
        S.emit("dve", lambda e: e.tensor_tensor(out=t[4][:], in0=zi1, in1=are, op=ALU.mult), reads=["ZI", "aT"], writes=["tq4"])
        S.emit("dve", lambda e: e.tensor_tensor(out=t[5][:], in0=t[2][:], in1=aim, op=ALU.mult), reads=["tq2", "aT"], writes=["tq5"])
        S.emit("dve", lambda e: e.tensor_tensor(out=t[4][:], in0=t[4][:], in1=t[5][:], op=ALU.subtract), reads=["tq4", "tq5"], writes=["tq4"])
        S.emit("dve", lambda e: e.tensor_tensor(out=qi[:], in0=t[4][:], in1=t[0][:], op=ALU.mult), reads=["tq4", "tq0"], writes=["qi"])

        W1 = T("W1", [128, 32, 16], F32)
        W2 = T("W2", [128, 32, 16], F32)
        tw = T("tw", [128, 32, 16], F32)
        qrb = qr[:].unsqueeze(2).broadcast_to([128, 32, 16])
        qib = qi[:].unsqueeze(2).broadcast_to([128, 32, 16])
        S.emit("dve", lambda e: e.tensor_tensor(out=W1[:], in0=BA[:], in1=qrb, op=ALU.mult), reads=BAk + ["qr"], writes=["W1"])
        S.emit("dve", lambda e: e.tensor_tensor(out=tw[:], in0=BB[:], in1=qib, op=ALU.mult), reads=BBk + ["qi"], writes=["tw"])
        S.emit("dve", lambda e: e.scalar_tensor_tensor(out=W1[:], in0=tw[:], scalar=SG[:, 0:1], in1=W1[:], op0=ALU.mult, op1=ALU.add),
               reads=["tw", "W1"] + SGk, writes=["W1"])
        S.emit("dve", lambda e: e.tensor_tensor(out=W2[:], in0=BB[:], in1=qrb, op=ALU.mult), reads=BBk + ["qr"], writes=["W2"])
        S.emit("dve", lambda e: e.tensor_scalar(out=W2[:], in0=W2[:], scalar1=SG[:, 0:1], scalar2=None, op0=ALU.mult), reads=["W2"] + SGk, writes=["W2"])
        S.emit("dve", lambda e: e.tensor_tensor(out=tw[:], in0=BA[:], in1=qib, op=ALU.mult), reads=BAk + ["qi"], writes=["tw"])
        S.emit("dve", lambda e: e.tensor_tensor(out=W2[:], in0=W2[:], in1=tw[:], op=ALU.subtract), reads=["W2", "tw"], writes=["W2"])

        stg = [T(f"stg{j}", [128, 1024], F32) for j in range(2)]
        sc = [0]

        def stage_cast(src_ap, ncols, dst_ap, scale_ap, dkey):
            j = sc[0] % 2
            sc[0] += 1
            half = ncols // 2
            S.emit("sp", lambda e: e.dma_start(out=stg[j][:, 0:ncols], in_=src_ap), writes=[("stg", j)], dma=f"ld_stg{j}")
            if scale_ap is None:
                S.emit("dve", lambda e: e.tensor_copy(out=dst_ap[:, 0:half], in_=stg[j][:, 0:half]), reads=[("stg", j)], writes=[(dkey, 0)])
                S.emit("dve", lambda e: e.tensor_copy(out=dst_ap[:, half:ncols], in_=stg[j][:, half:ncols]),
                       reads=[("stg", j)], writes=[(dkey, 1)])
            else:
                S.emit("dve", lambda e: e.tensor_scalar(out=dst_ap[:, 0:half], in0=stg[j][:, 0:half], scalar1=scale_ap, scalar2=None, op0=ALU.mult),
                       reads=[("stg", j), "gcol"], writes=[(dkey, 0)])
                S.emit("dve", lambda e: e.tensor_scalar(out=dst_ap[:, half:ncols], in0=stg[j][:, half:ncols], scalar1=scale_ap, scalar2=None, op0=ALU.mult),
                       reads=[("stg", j), "gcol"], writes=[(dkey, 1)])

        wq = []
        for k in range(8):
            for c3 in range(3):
                wq.append(lambda k=k, c3=c3: stage_cast(w_in[128 * k:128 * k + 128, 1024 * c3:1024 * c3 + 1024], 1024,
                                                        w_in_bf[:, k, 1024 * c3:1024 * c3 + 1024], smallT[:, k:k + 1], ("w_in", k, c3)))
        for k in range(8):
            wq.append(lambda k=k: stage_cast(w_out[128 * k:128 * k + 128, :], 1024, w_out_bf[:, k, :], None, ("w_out", k)))
        for k in range(4):
            wq.append(lambda k=k: stage_cast(w_glu[128 * k:128 * k + 128, :], 512, w_glu_bf[:, k, :], None, ("w_glu", k)))

        def wq_pop(n=1):
            for _ in range(n):
                if wq:
                    wq.pop(0)()


        Bbig = T("Bbig", [128, 32, 8, 16], F32)
        Bneg = T("Bneg", [128, 32, 8, 16], F32)
        tb = T("tb", [128, 32, 16], F32)
        CbTv = CbT[:].rearrange("p g (r h) -> p g r h", r=8, h=16)
        for s in range(8):
            for (dst, dk, ev, M1, M2, m1k, m2k, op2) in (
                    (Bbig, ("Bbig", s), 7 - s, W1, W2, "W1", "W2", ALU.add),
                    (Bneg, ("Bneg", s), -1 - s, W1, W2, "W1", "W2", ALU.add),
                    (None, ("CbT", s), s + 1, V1, V2, "V1", "V2", ALU.subtract)):
                dv = CbTv[:, :, s, :] if dst is None else dst[:, :, s, :]
                zrb = ZR[:, :, eidx(ev)].unsqueeze(2).broadcast_to([128, 32, 16])
                zib = ZI[:, :, eidx(ev)].unsqueeze(2).broadcast_to([128, 32, 16])
                S.emit("dve", lambda e, dv=dv, M1=M1, zrb=zrb: e.tensor_tensor(out=dv, in0=M1[:], in1=zrb, op=ALU.mult),
                       reads=[m1k, "ZR"], writes=[dk])
                S.emit("pool", lambda e, M2=M2, zib=zib: e.tensor_tensor(out=tb[:], in0=M2[:], in1=zib, op=ALU.mult),
                       reads=[m2k, "ZI"], writes=["tb"])
                S.emit("dve", lambda e, dv=dv, op2=op2: e.tensor_tensor(out=dv, in0=dv, in1=tb[:], op=op2),
                       reads=[dk, "tb"], writes=[dk])
                wq_pop(1)
        Bbigk = [("Bbig", s) for s in range(8)]
        Bnegk = [("Bneg", s) for s in range(8)]
        CbTk = [("CbT", s) for s in range(8)]

        S.emit("dve", lambda e: e.tensor_copy(out=RHO0[:], in_=MAG[:, :, eidx(8)]), reads=["MAG"], writes=["RHO0"])
        S.emit("dve", lambda e: e.tensor_copy(out=COSm[:, :, 0], in_=COSt[:, :, eidx(8)]), reads=["COSt"], writes=[("COSm", 0)])
        S.emit("dve", lambda e: e.tensor_copy(out=SINm[:, :, 0], in_=SINt[:, :, eidx(8)]), reads=["SINt"], writes=[("SINm", 0)])
        CbTbk = CbTk

        tkv = tw[:].rearrange("p (a b) h -> p a (b h)", a=4, b=8)
        for q4 in range(8):
            wq_pop(1)
            pp, ppk = next_pp()
            for j in range(4):
                g = 4 * q4 + j
                S.emit("pe", lambda e, g=g, j=j, pp=pp: e.transpose(out=pp[:, 128 * j:128 * j + 128],
                                                                   in_=Bbig[:, g].rearrange("p s h -> p (s h)"), identity=ident[:]),
                       reads=Bbigk + ["ident"], writes=[ppk])
            ppv = pp[:].rearrange("p (g k) -> p g k", g=4)
            S.emit("act", lambda e, q4=q4, pp=pp: e.activation(out=BbigT[:, 4 * q4:4 * q4 + 4, :].rearrange("p g k -> p (g k)"), in_=pp[:], func=AF.Copy),
                   reads=[ppk], writes=[("BbigT", q4)])
            S.emit("dve", lambda e, q4=q4, ppv=ppv: e.tensor_scalar(out=BbigswT[:, 4 * q4:4 * q4 + 4, 0:64], in0=ppv[:, :, 64:128], scalar1=-1.0, scalar2=None, op0=ALU.mult),
                   reads=[ppk], writes=[("BbigswT", q4, 0)])
            S.emit("dve", lambda e, q4=q4, ppv=ppv: e.tensor_copy(out=BbigswT[:, 4 * q4:4 * q4 + 4, 64:128], in_=ppv[:, :, 0:64]),
                   reads=[ppk], writes=[("BbigswT", q4, 1)])
            pp, ppk = next_pp()
            for j in range(4):
                g = 4 * q4 + j
                S.emit("pe", lambda e, g=g, j=j, pp=pp: e.matmul(pp[:, 128 * j:128 * j + 128], lhsT=Bneg[:, g].rearrange("p s h -> p (s h)"),
                                                                rhs=CbT[:, g, :], start=True, stop=True),
                       reads=Bnegk + CbTk, writes=[ppk])
            S.emit("dve", lambda e, pp=pp: e.tensor_tensor(out=tkv, in0=pp[:].rearrange("p (g k) -> p g k", g=4),
                                                          in1=mask[:].unsqueeze(1).broadcast_to([128, 4, 128]), op=ALU.mult),
                   reads=[ppk, "mask"], writes=["tw"])
            for j in range(4):
                g = 4 * q4 + j
                S.emit("dve", lambda e, g=g, j=j: e.scalar_tensor_tensor(out=Kb0T[:, g, :], in0=ident[:], scalar=dcol[:, g:g + 1], in1=tkv[:, j, :],
                                                                        op0=ALU.mult, op1=ALU.add),
                       reads=["ident", "dcol", "tw"], writes=[("Kb0T", g)])

        wq_pop(100)

        def wk(name, k):
            if name == "w_in":
                return [((name, k, c3), h) for c3 in range(3) for h in range(2)]
            return [((name, k), 0), ((name, k), 1)]

        S.barrier()
        st_setup.close()
        tstack[0] = st
        xin = [T(f"xin{j}", [128, 1024], F32) for j in range(2)]
        xnb = [T(f"xnb{j}", [128, 1024], BF16) for j in range(2)]
        ssq = [T(f"ssq{j}", [128, 1], F32) for j in range(2)]
        rstd = [T(f"rstd{j}", [128, 1], F32) for j in range(2)]
        xnT = T("xnT", [128, 8, 512], BF16)
        UTbuf = T("UTbuf", [128, 4096], BF16)
        UT = UTbuf[:, 0:2048].rearrange("p (g k h) -> p g k h", g=32, k=4, h=16)
        GT = UTbuf[0:64, :].rearrange("p (a k) -> p a k", a=32, k=128)
        U_st = T("U_st", [128, 32, 64], BF16)
        gy = T("gy", [128, 4, 512], BF16)
        zs = T("zs", [128, 4, 512], BF16)
        yconv = T("yconv", [128, 4, 512], BF16)
        vb0 = T("vb0", [128, 512], F32)
        vb = [vb0, vb0]
        carry = T("carry", [128, 4, 2], F32)
        csb = T("csb", [128, 512], F32)
        acc = T("acc", [128, 512], F32)
        szt = T("szt", [128, 512], F32)
        r1 = T("r1", [128, 512], F32)
        r2 = T("r2", [128, 512], F32)
        bt = T("bt", [128, 512], F32)
        xt = T("xt", [128, 512], F32)
        Xb0 = T("Xb0", [128, 8, 64], BF16)
        Xb = [Xb0, Xb0]
        CbTb = T("CbTb", [128, 32, 128], BF16)
        Xc = T("Xc", [128, 32], F32)
        Y_st = [Xb0, Xb0]
        sig = szt
        dmy = T("dmy", [128, 2], F32)
        xres = [T(f"xres{j}", [128, 1024], F32) for j in range(2)]
        ssq2 = [T(f"ssq2{j}", [128, 1], F32) for j in range(2)]
        rstd2 = [T(f"rstd2{j}", [128, 1], F32) for j in range(2)]

        CbTbk2 = [("CbTb", q_) for q_ in range(4)]

        def build_rot_tables():
            for q_ in range(4):
                S.emit("dve", lambda e, q_=q_: e.tensor_copy(out=CbTb[:, 8 * q_:8 * q_ + 8, :], in_=CbT[:, 8 * q_:8 * q_ + 8, :]), reads=CbTk, writes=[("CbTb", q_)])
            rtm = [t_[:].rearrange("p (g c) -> p g c", g=32) for t_ in (r1, r2, bt, xt)]
            rtk = ("r1", "r2", "bt", "xt")
            for lvl, n in enumerate((1, 2, 4, 8, 16)):
                cmul_block("dve", "pool", COSm, SINm, "COSm", "SINm", 0, n, lvl, rtm, tkeys=rtk)
            cmul_block("dve", "pool", COSm, SINm, "COSm", "SINm", 0, 16, 5, rtm, src0=0, dst0=32, midx=31, tkeys=rtk, wlvl=6)
            cmul_block("dve", "pool", COSm, SINm, "COSm", "SINm", 0, 16, 5, rtm, src0=16, dst0=48, midx=31, tkeys=rtk, wlvl=7)
            dm2 = T("dm2", [128, 2], F32)
            S.emit("dve", lambda e: e.memset(dm2[:, 0:1], 0.0), reads=[("COSm", l) for l in range(8)] + ["r1", "r2"], writes=["COSm"])
            S.emit("pool", lambda e: e.memset(dm2[:, 1:2], 0.0), reads=[("SINm", l) for l in range(8)] + ["bt", "xt"], writes=["SINm"])


        tile_ctr = [0]
        out_tokens = []

        def load_rows(dst, view, sg, k, key, sem, eng="sp"):
            fns = []
            for sl in range(2):
                fns.append(lambda e, sl=sl: e.dma_start(out=dst[64 * sl:64 * sl + 64, :], in_=view[sg, k + 4 * sl]))
            S.emit_group(eng, fns, writes=[key], dma=sem)
            return [key]

        def rms_rstd(src, srck, ssq_t, rstd_t, key, junk, junkk):
            S.emit("pool", lambda e: e.memset(ssq_t[:], 0.0), writes=[key + "_ssq"])
            S.emit("act", lambda e: e.activation(out=junk[:], in_=src[:], func=AF.Square, accum_out=ssq_t[:, 0:1]),
                   reads=srck + [key + "_ssq"], writes=[junkk, key + "_ssq"])
            S.emit("act", lambda e: e.activation(out=rstd_t[:], in_=ssq_t[:], func=AF.Sqrt, scale=1.0 / 1024.0, bias=epsT[:, 0:1]),
                   reads=[key + "_ssq", "epsT"], writes=[key + "_rstd"])
            S.emit("dve", lambda e: e.reciprocal(out=rstd_t[:], in_=rstd_t[:]), reads=[key + "_rstd"], writes=[key + "_rstd"])

        xnTk = [("xnT", k) for k in range(4)]
        UTk = [("UT", k) for k in range(4)]

        s1state = {}

        def seg_stage1(sg, part):
            if part == 1:
                js = []
                for k in range(4):
                    js.append(tile_ctr[0] % 2)
                    tile_ctr[0] += 1
                s1state[sg] = dict(js=js, xks={}, pts={})
            js, xks, pts = s1state[sg]["js"], s1state[sg]["xks"], s1state[sg]["pts"]

            def sA(k):
                j = js[k]
                xks[k] = load_rows(xin[j], xv, sg, k, f"xin{j}", f"ld_x{j}")
                rms_rstd(xin[j], xks[k], ssq[j], rstd[j], f"n1_{j}", xnb[j], f"xnb{j}")

            def sB(k):
                j = js[k]
                S.emit("dve", lambda e, j=j: e.tensor_scalar(out=xnb[j][:], in0=xin[j][:], scalar1=rstd[j][:, 0:1], scalar2=None, op0=ALU.mult),
                       reads=xks[k] + [f"n1_{j}_rstd"], writes=[f"xnb{j}"])

            def sC(k):
                j = js[k]
                pt, ptk = next_pt()
                pts[k] = (pt, ptk)
                for dc in range(8):
                    S.emit("pe", lambda e, j=j, dc=dc, pt=pt: e.transpose(out=pt[:, 128 * dc:128 * dc + 128], in_=xnb[j][:, 128 * dc:128 * dc + 128], identity=identb[:]),
                           reads=[f"xnb{j}", "identb"], writes=[ptk])

            def sD(k):
                pt, ptk = pts[k]
                S.emit("act", lambda e, k=k, pt=pt: e.activation(out=xnT[:, :, 128 * k:128 * k + 128], in_=pt[:].rearrange("p (a b) -> p a b", a=8), func=AF.Copy),
                       reads=[ptk], writes=[("xnT", k)])

            if part == 1:
                sA(0); sA(1); sB(0); sB(1)
            elif part == 2:
                sC(0); sC(1); sD(0); sA(2); sD(1); sA(3); sB(2); sB(3)
            else:
                sC(2); sC(3); sD(2); sD(3)

        def seg_main(sg):
            first = (sg % 4 == 0)

            def proj(col_chunk):
                pp, ppk = next_pp()
                for dc in range(8):
                    S.emit("pe", lambda e, dc=dc, pp=pp: e.matmul(pp[:], lhsT=w_in_bf[:, dc, 128 * col_chunk:128 * col_chunk + 128], rhs=xnT[:, dc, :],
                                                                 start=(dc == 0), stop=(dc == 7)),
                           reads=xnTk + wk("w_in", dc), writes=[ppk])
                return pp, ppk

            for k in range(4):
                pp, ppk = next_pp()
                for dc in range(8):
                    S.emit("pe", lambda e, dc=dc, pp=pp, k=k: e.matmul(pp[:], lhsT=xnT[:, dc, 128 * k:128 * k + 128], rhs=w_in_bf[:, dc, 0:512],
                                                                      start=(dc == 0), stop=(dc == 7)),
                           reads=[("xnT", k)] + wk("w_in", dc), writes=[ppk])
                if k % 2 == 0:
                    S.emit("act", lambda e, pp=pp, k=k: e.activation(out=UT[:, :, k, :], in_=pp[:].rearrange("p (g h) -> p g h", g=32), func=AF.Copy),
                           reads=[ppk], writes=[("UT", k)])
                else:
                    S.emit("dve", lambda e, pp=pp, k=k: e.tensor_copy(out=UT[:, :, k, :], in_=pp[:].rearrange("p (g h) -> p g h", g=32)),
                           reads=[ppk], writes=[("UT", k)])
            pz0, pz0k = proj(4)
            S.emit("act", lambda e, pz0=pz0: e.activation(out=zs[:, 0, :], in_=pz0[:], func=AF.Silu), reads=[pz0k], writes=[("zs", 0)])
            for hb in range(2):
                pt, ptk = next_pt()
                for gg in range(16):
                    g = 16 * hb + gg
                    for sl in range(2):
                        S.emit("pe", lambda e, g=g, gg=gg, pt=pt, sl=sl: e.transpose(out=pt[64 * sl:64 * sl + 64, 64 * gg:64 * gg + 64],
                                                                                   in_=UT[64 * sl:64 * sl + 64, g, :, :].rearrange("p k h -> p (k h)"),
                                                                                   identity=identb[64 * sl:64 * sl + 64, 64 * sl:64 * sl + 64]),
                               reads=UTk + ["identb"], writes=[ptk])
                if hb == 0:
                    S.emit("act", lambda e, hb=hb, pt=pt: e.activation(out=U_st[:, 16 * hb:16 * hb + 16, :].rearrange("p g c -> p (g c)"), in_=pt[:], func=AF.Copy),
                           reads=[ptk], writes=[("U_st", 2 * hb), ("U_st", 2 * hb + 1)])
                else:
                    S.emit("dve", lambda e, hb=hb, pt=pt: e.tensor_copy(out=U_st[:, 16 * hb:16 * hb + 16, :].rearrange("p g c -> p (g c)"), in_=pt[:]),
                           reads=[ptk], writes=[("U_st", 2 * hb), ("U_st", 2 * hb + 1)])

            if sg == 0:
                S.emit("pool", lambda e: e.memset(dmy[:], 0.0), writes=["dmy0", "dmy1"])
            if first:
                S.emit("pool", lambda e: e.memset(Xc[:], 0.0), writes=[("Xc", b) for b in range(4)])
                S.emit("pool", lambda e: e.memset(carry[:], 0.0), writes=[("carry", i) for i in range(4)])

            stt = {}

            def fma(dst, src, w, rk):
                S.emit("dve", lambda e: e.scalar_tensor_tensor(out=dst, in0=src, scalar=w, in1=dst, op0=ALU.mult, op1=ALU.add),
                       reads=rk + convwk + ["acc"], writes=["acc"])

            def v2(ap, off, n):
                return bass.AP(ap.tensor, ap.offset + off, [list(ap.ap[0]), [64, 2], [1, n]])

            def stA(b):
                ps, psk = next_pp()
                psw, pswk = next_pp()
                for gg in range(8):
                    g = 8 * b + gg
                    S.emit("pe", lambda e, ps=ps, g=g, gg=gg: e.matmul(ps[:, 64 * gg:64 * gg + 64], lhsT=BbigT[:, g, :], rhs=U_st[:, g, :], start=True, stop=True),
                           reads=[("U_st", b), ("BbigT", g // 4)], writes=[psk])
                for gg in range(8):
                    g = 8 * b + gg
                    S.emit("pe", lambda e, psw=psw, g=g, gg=gg: e.matmul(psw[:, 64 * gg:64 * gg + 64], lhsT=BbigswT[:, g, :], rhs=U_st[:, g, :], start=True, stop=True),
                           reads=[("U_st", b), ("BbigswT", g // 4, 0), ("BbigswT", g // 4, 1)], writes=[pswk])
                stt[b] = dict(ps=ps, psk=psk, psw=psw, pswk=pswk)

            def stZ(b):
                pz, pzk = proj(4 + b)
                S.emit("act", lambda e, b=b, pz=pz: e.activation(out=zs[:, b, :], in_=pz[:], func=AF.Silu), reads=[pzk], writes=[("zs", b)])

            def stR(b):
                d = stt[b]
                gs = slice(8 * b, 8 * b + 8)
                cosb = COSm[:, gs, :].rearrange("p g c -> p (g c)")
                sinb = SINm[:, gs, :].rearrange("p g c -> p (g c)")
                d["cosb"], d["sinb"], d["gs"] = cosb, sinb, gs
                ps, psw = d["ps"], d["psw"]
                S.emit("dve", lambda e, ps=ps, cosb=cosb: e.tensor_tensor(out=r1[:], in0=ps[:], in1=cosb, op=ALU.mult), reads=[d["psk"], "COSm"], writes=["r1"])
                S.emit("dve", lambda e, psw=psw, sinb=sinb: e.tensor_tensor(out=r2[:], in0=psw[:], in1=sinb, op=ALU.mult), reads=[d["pswk"], "SINm"], writes=["r2"])
                S.emit("pool", lambda e: e.tensor_tensor(out=bt[:], in0=r1[:], in1=r2[:], op=ALU.subtract), reads=["r1", "r2"], writes=["bt"])
                for gg in range(8):
                    g = 8 * b + gg
                    S.emit("dve", lambda e, g=g, gg=gg: e.tensor_tensor_scan(out=xt[:, 64 * gg:64 * gg + 64], data0=RHO0[:, g:g + 1].broadcast_to([128, 64]),
                                                                           data1=bt[:, 64 * gg:64 * gg + 64], initial=Xc[:, g:g + 1], op0=ALU.mult, op1=ALU.add),
                           reads=["bt", "RHO0", ("Xc", b)], writes=[("xt", gg)])

            def stC1(b):
                i = b
                ph, phk = proj(8 + i)
                pc, pck = proj(16 + i)
                S.emit("act", lambda e, pc=pc: e.activation(out=csb[:], in_=pc[:], func=AF.Copy), reads=[pck], writes=["csb"])
                v = vb[0]
                vk_ = ("vb", 0)
                S.emit("dve", lambda e, v=v, ph=ph: e.tensor_tensor(out=v[:], in0=ph[:], in1=csb[:], op=ALU.mult), reads=[phk, "csb"], writes=[vk_])
                w2 = smallT[:, 20 + i:21 + i]
                S.emit("act", lambda e, v=v, w2=w2: e.activation(out=acc[:], in_=v[:], func=AF.Copy, scale=w2), reads=[vk_] + convwk, writes=["acc"])

            def stX(b):
                d = stt[b]
                gs, cosb, sinb = d["gs"], d["cosb"], d["sinb"]
                xtk = [("xt", gg) for gg in range(8)]
                px, pxk = next_pp()
                S.emit("pe", lambda e, px=px: e.matmul(px[:], lhsT=PswT[:], rhs=xt[:], start=True, stop=True), reads=xtk + ["PswT"], writes=[pxk])
                S.emit("pool", lambda e, cosb=cosb: e.tensor_tensor(out=r1[:], in0=xt[:], in1=cosb, op=ALU.mult), reads=xtk + ["COSm"], writes=["r1"])
                S.emit("dve", lambda e, px=px, sinb=sinb: e.tensor_tensor(out=r2[:], in0=px[:], in1=sinb, op=ALU.mult), reads=[pxk, "SINm"], writes=["r2"])
                xb = Xb[b % 2]
                xbk = ("Xb", 0)
                d["xb"], d["xbk"] = xb, xbk
                r1v = r1[:].rearrange("p (g c) -> p g c", g=8)
                r2v = r2[:].rearrange("p (g c) -> p g c", g=8)
                S.emit("dve", lambda e, xb=xb, r1v=r1v, r2v=r2v: e.tensor_tensor(out=xb[:, :, 1:64], in0=r1v[:, :, 0:63], in1=r2v[:, :, 0:63], op=ALU.add),
                       reads=["r1", "r2"], writes=[xbk])
                S.emit("dve", lambda e, xb=xb, gs=gs: e.tensor_copy(out=xb[:, :, 0], in_=Xc[:, gs]), reads=[("Xc", b), xbk], writes=[xbk])
                S.emit("dve", lambda e, gs=gs, r1v=r1v, r2v=r2v: e.tensor_tensor(out=Xc[:, gs], in0=r1v[:, :, 63], in1=r2v[:, :, 63], op=ALU.add),
                       reads=["r1", "r2"], writes=[("Xc", b)])

            def stC2(b):
                i = b
                v = vb[0]
                vk_, vpk = ("vb", 0), ("carry", i)
                w0, w1 = smallT[:, 12 + i:13 + i], smallT[:, 16 + i:17 + i]
                def v3(ap, off, n):
                    return bass.AP(ap.tensor, ap.offset + off, [list(ap.ap[0]), [128, 2], [1, n]])

                fma(acc[:, 128:512], v[:, 0:384], w1, [vk_])
                fma(acc[:, 64:128], v[:, 384:448], w1, [vk_])
                fma(acc[:, 1:64], v[:, 448:511], w1, [vk_])
                fma(acc[:, 0:1], carry[:, i, 1:2], w1, [vpk])
                fma(acc[:, 256:512], v[:, 0:256], w0, [vk_])
                fma(v3(acc[:], 64, 64), v3(v[:], 256, 64), w0, [vk_])
                fma(v3(acc[:], 1, 63), v3(v[:], 320, 63), w0, [vk_])
                fma(bass.AP(acc[:].tensor, acc[:].offset, [list(acc[:].ap[0]), [128, 2]]), carry[:, i, :], w0, [vpk])
                S.emit("pool", lambda e, v=v, i=i: e.tensor_copy(out=carry[:, i, :], in_=bass.AP(v[:].tensor, v[:].offset + 383, [list(v[:].ap[0]), [128, 2]])),
                       reads=[vk_], writes=[vpk])

            def stC3(b):
                i = b
                pb, pbk = proj(12 + i)
                pzc, pzck = proj(20 + i)
                S.emit("act", lambda e, pzc=pzc: e.activation(out=szt[:], in_=pzc[:], func=AF.Silu), reads=[pzck], writes=["szt"])
                S.emit("dve", lambda e, pb=pb: e.tensor_tensor(out=acc[:], in0=pb[:], in1=acc[:], op=ALU.mult), reads=[pbk, "acc"], writes=["acc"])
                S.emit("pool", lambda e, i=i: e.tensor_tensor(out=yconv[:, i, :], in0=acc[:], in1=szt[:], op=ALU.mult), reads=["acc", "szt"], writes=[("yconv", i)])

            def stY(b):
                d = stt[b]
                xb, xbk = d["xb"], d["xbk"]
                py, pyk = next_pp()
                for gg in range(8):
                    g = 8 * b + gg
                    S.emit("pe", lambda e, py=py, g=g, gg=gg: e.matmul(py[:, 64 * gg:64 * gg + 64], lhsT=Kb0T[:, g, :], rhs=U_st[:, g, :], start=True, stop=False),
                           reads=[("U_st", b), ("Kb0T", g)], writes=[pyk])
                    S.emit("pe", lambda e, py=py, g=g, gg=gg, xb=xb: e.matmul(py[:, 64 * gg:64 * gg + 64], lhsT=CbTb[:, g, :], rhs=xb[:, gg, :], start=False, stop=True),
                           reads=CbTbk2 + [xbk], writes=[pyk])
                yst = Y_st[0]
                ystk = ("Xb", 0)
                S.emit("act", lambda e, py=py, yst=yst: e.activation(out=yst[:].rearrange("p g c -> p (g c)"), in_=py[:], func=AF.Copy), reads=[pyk], writes=[ystk])
                S.emit("act", lambda e: e.activation(out=dmy[:, 1:2], in_=dmy[:, 0:1], func=AF.Gelu_apprx_tanh), reads=["dmy0"], writes=["dmy1"])

            GTv = GT.rearrange("p (r b2) k -> p r b2 k", r=8, b2=4)

            def stT1(b):
                yst = Y_st[0]
                ystk = ("Xb", 0)
                pt, ptk = next_pt()
                for gg in range(8):
                    S.emit("pe", lambda e, gg=gg, pt=pt, yst=yst: e.transpose(out=pt[0:64, 128 * gg:128 * gg + 128], in_=yst[:, gg, :], identity=identb[:]),
                           reads=[ystk, "identb"], writes=[ptk])
                S.emit("act", lambda e, pt=pt, b=b: e.activation(out=GTv[:, :, b, :].rearrange("p r (g h) -> p r g h", g=8),
                                                               in_=pt[0:64, :].rearrange("p (g r h) -> p r g h", g=8, r=8, h=16), func=AF.Gelu_apprx_tanh),
                       reads=[ptk], writes=[("GT", b)] + UTk)

            def stT2(b):
                pt2, pt2k = next_pt()
                for r in range(8):
                    co = (r % 4) * 128 + (r // 4) * 64
                    S.emit("pe", lambda e, r=r, b=b, pt2=pt2, co=co: e.transpose(out=pt2[:, co:co + 64], in_=GTv[:, r, b, :], identity=identb[0:64, 0:64]),
                           reads=[("GT", b), "identb"] + UTk, writes=[pt2k])
                S.emit("dve", lambda e, pt2=pt2, b=b: e.tensor_copy(out=gy[:, b, :], in_=pt2[:, 0:512]), reads=[pt2k], writes=[("gy", b)])

            if sg == 0:
                build_rot_tables()
            stA(0)
            stR(0)
            for b in range(4):
                stC1(b)
                stX(b)
                stC2(b)
                stC3(b)
                stY(b)
                if b < 3:
                    stA(b + 1)
                stT1(b)
                if b < 3:
                    stZ(b + 1)
                    stR(b + 1)
                stT2(b)

        def seg_tail(sg, part):
            if part == 2:
                return seg_tail_out(sg, (0, 1))
            if part == 3:
                return seg_tail_out(sg, (2, 3))
            gyk = [("gy", b) for b in range(4)]
            if dbg and sg == dbg - 1:
                items = [("dbg_U", U_st, BF16, [128, 32, 64], [("U_st", b) for b in range(4)]),
                         ("dbg_gy", gy, BF16, [128, 4, 512], gyk),
                         ("dbg_yc", yconv, BF16, [128, 4, 512], [("yconv", b) for b in range(4)]),
                         ("dbg_COS", COSm, F32, [128, 32, 64], ["COSm"]),
                         ("dbg_SIN", SINm, F32, [128, 32, 64], ["SINm"]),
                         ("dbg_RHO", RHO0, F32, [128, 32], ["RHO0"]),
                         ("dbg_Xc", Xc, F32, [128, 32], [("Xc", b) for b in range(4)]),
                         ("dbg_BbigT", BbigT, BF16, [128, 32, 128], [("BbigT", q) for q in range(8)]),
                         ("dbg_BbigswT", BbigswT, BF16, [128, 32, 128], [("BbigswT", q, h) for q in range(8) for h in range(2)]),
                         ("dbg_Kb0T", Kb0T, BF16, [128, 32, 128], [("Kb0T", g) for g in range(32)]),
                         ("dbg_CbT", CbT, F32, [128, 32, 128], CbTk)]
                for (nm, tl, dt_, shp, rk) in items:
                    dap = nc.dram_tensor(nm, shp, dt_, kind="ExternalOutput").ap()
                    out_tokens.append(S.emit("sp", lambda e, dap=dap, tl=tl: e.dma_start(out=dap, in_=tl[:]), reads=rk, dma="st_dbg_" + nm))
            for eo in range(4):
                pp, ppk = next_pp()
                for ch in range(4):
                    S.emit("pe", lambda e, pp=pp, ch=ch, eo=eo: e.matmul(pp[:], lhsT=w_glu_bf[:, ch, 128 * eo:128 * eo + 128], rhs=gy[:, ch, :],
                                                                        start=(ch == 0), stop=(ch == 3)),
                           reads=gyk + wk("w_glu", ch), writes=[ppk])
                S.emit("act", lambda e, pp=pp, eo=eo: e.activation(out=sig[:], in_=pp[:], func=AF.Sigmoid, bias=smallT[:, 8 + eo:9 + eo]),
                       reads=[ppk, "bglu"], writes=["szt"])
                S.emit("dve", lambda e, eo=eo: e.tensor_tensor(out=sig[:], in0=gy[:, eo, :], in1=sig[:], op=ALU.mult), reads=[("gy", eo), "szt"], writes=["szt"])
                S.emit("dve", lambda e, eo=eo: e.tensor_tensor(out=zs[:, eo, :], in0=sig[:], in1=zs[:, eo, :], op=ALU.mult),
                       reads=["szt", ("zs", eo)], writes=[("zs", eo)])

        def seg_tail_out(sg, tiles):
            ys = zs
            junk2 = sig[:].bitcast(BF16)
            info = {}
            for k in tiles:
                j = (sg * 4 + k) % 2
                info[k] = dict(j=j, xrk=load_rows(xres[j], xv, sg, k, f"xres{j}", f"ld_r{j}", eng="pool"))
            for k in tiles:
                info[k]["pos"] = [next_pp() for dh in range(2)]
            for chs in ((4, 5, 6, 7), (0, 1, 2, 3)):
                for k in tiles:
                    for dh in range(2):
                        pp, ppk = info[k]["pos"][dh]
                        for ch in chs:
                            lhsT = ys[:, ch, 128 * k:128 * k + 128] if ch < 4 else yconv[:, ch - 4, 128 * k:128 * k + 128]
                            rk = [("zs", ch)] if ch < 4 else [("yconv", ch - 4)]
                            S.emit("pe", lambda e, pp=pp, lhsT=lhsT, ch=ch, dh=dh: e.matmul(pp[:], lhsT=lhsT, rhs=w_out_bf[:, ch, 512 * dh:512 * dh + 512],
                                                                                           start=(ch == 4), stop=(ch == 3)),
                                   reads=rk + wk("w_out", ch), writes=[ppk])
            for k in tiles:
                j, xrk = info[k]["j"], info[k]["xrk"]
                for dh in range(2):
                    pp, ppk = info[k]["pos"][dh]
                    S.emit("dve", lambda e, pp=pp, dh=dh, j=j: e.tensor_tensor(out=xres[j][:, 512 * dh:512 * dh + 512], in0=pp[:],
                                                                              in1=xres[j][:, 512 * dh:512 * dh + 512], op=ALU.add),
                           reads=[ppk] + xrk, writes=xrk)
            for k in tiles:
                j, xrk = info[k]["j"], info[k]["xrk"]
                key = f"n2_{j}"
                S.emit("pool", lambda e, j=j: e.memset(ssq2[j][:], 0.0), writes=[key + "_ssq"])
                S.emit("act", lambda e, j=j: e.activation(out=junk2, in_=xres[j][:], func=AF.Square, accum_out=ssq2[j][:, 0:1]),
                       reads=xrk + [key + "_ssq"], writes=["szt", key + "_ssq"])
                S.emit("act", lambda e, j=j: e.activation(out=rstd2[j][:], in_=ssq2[j][:], func=AF.Sqrt, scale=1.0 / 1024.0, bias=epsT[:, 0:1]),
                       reads=[key + "_ssq", "epsT"], writes=[key + "_rstd"])
            for k in tiles:
                j, xrk = info[k]["j"], info[k]["xrk"]
                key = f"n2_{j}"
                S.emit("dve", lambda e, j=j: e.reciprocal(out=rstd2[j][:], in_=rstd2[j][:]), reads=[key + "_rstd"], writes=[key + "_rstd"])
                S.emit("dve", lambda e, j=j: e.scalar_tensor_tensor(out=xres[j][:], in0=xres[j][:], scalar=rstd2[j][:, 0:1], in1=fg[:],
                                                                   op0=ALU.mult, op1=ALU.mult),
                       reads=xrk + [key + "_rstd", "fg"], writes=xrk)
                fns = []
                for sl in range(2):
                    fns.append(lambda e, sl=sl, k=k, sg=sg, j=j: e.dma_start(out=ov[sg, k + 4 * sl], in_=xres[j][64 * sl:64 * sl + 64, :]))
                out_tokens.append(S.emit_group("sp", fns, reads=xrk, dma=f"st_o{j}"))

        seg_stage1(0, 1)
        seg_stage1(0, 2)
        seg_stage1(0, 3)
        for sg in range(nseg):
            nxt = sg + 1 < nseg
            seg_main(sg)
            if nxt:
                seg_stage1(sg + 1, 1)
            seg_tail(sg, 1)
            if nxt:
                seg_stage1(sg + 1, 2)
            seg_tail(sg, 2)
            if nxt:
                seg_stage1(sg + 1, 3)
            seg_tail(sg, 3)
        S.finish(out_tokens)
    return nc


_CACHE = {}


def _get_nc():
    if "nc" not in _CACHE:
        _CACHE["nc"] = build_program()
    return _CACHE["nc"]


def _in_maps(inputs):
    f = lambda a: np.ascontiguousarray(np.asarray(a, dtype=np.float32))
    x = f(inputs["x"])
    shared = {
        "norm_gain": f(inputs["norm_gain"]).reshape(1024),
        "w_in": f(inputs["w_in"]).reshape(1024, 3072),
        "ssm_a_re": f(inputs["ssm_a_re"]).reshape(32, 64),
        "ssm_a_im": f(inputs["ssm_a_im"]).reshape(32, 64),
        "ssm_log_dt": f(inputs["ssm_log_dt"]).reshape(32),
        "ssm_b_re": f(inputs["ssm_b_re"]).reshape(32, 64, 16),
        "ssm_b_im": f(inputs["ssm_b_im"]).reshape(32, 64, 16),
        "ssm_c_re": f(inputs["ssm_c_re"]).reshape(32, 16, 64),
        "ssm_c_im": f(inputs["ssm_c_im"]).reshape(32, 16, 64),
        "ssm_d": f(inputs["ssm_d"]).reshape(32, 16),
        "w_glu": f(inputs["w_glu"]).reshape(512, 512),
        "b_glu": f(inputs["b_glu"]).reshape(512),
        "conv_w": f(inputs["conv_w"]).reshape(3, 512),
        "w_out": f(inputs["w_out"]).reshape(1024, 1024),
        "final_norm_gain": f(inputs["final_norm_gain"]).reshape(1024),
    }
    maps = []
    for r in range(NCORES):
        m = dict(shared)
        m["x"] = np.ascontiguousarray(x[2 * r:2 * r + 2].reshape(4096, 1024))
        maps.append(m)
    return maps


def kernel(**inputs):
    nc = _get_nc()
    res = run_bass_kernel_spmd(nc, _in_maps(inputs), core_ids=list(range(NCORES)))
    outs = [np.asarray(r["out"], dtype=np.float32).reshape(2, 2048, 1024) for r in res.results]
    return np.concatenate(outs, axis=0)
```

```python
import math
from contextlib import ExitStack
import numpy as np
import concourse.bass as bass
import concourse.mybir as mybir
from concourse.bass_utils import run_bass_kernel_spmd

F32 = mybir.dt.float32
BF16 = mybir.dt.bfloat16
ALU = mybir.AluOpType
AF = mybir.ActivationFunctionType

NCORES = 8
NSEG = 8
EPS = 1e-6


class Sched:
    ENG = ("pe", "act", "dve", "pool", "sp")

    def __init__(self, nc, stack):
        self.nc = nc
        self.stack = stack
        self.prog = {e: [] for e in self.ENG}
        self.count = {e: 0 for e in self.ENG}
        self.waited = {e: {} for e in self.ENG}
        self.sems = {}
        self.semval = {}
        self.res_w = {}
        self.res_r = {}
        self.pe_inc = set()
        for e in self.ENG:
            self._sem("eng_" + e)

    def _sem(self, name):
        if name not in self.sems:
            self.sems[name] = self.stack.enter_context(self.nc.semaphore(name))
            self.semval[name] = 0
        return self.sems[name]

    def _wait(self, eng, tok):
        if tok is None:
            return
        name, val = tok
        if eng == "pe" and name == "eng_pe":
            return
        if self.waited[eng].get(name, 0) >= val:
            return
        self.waited[eng][name] = val
        if name == "eng_pe":
            self.pe_inc.add(val)
        self.prog[eng].append(("wait", name, val))

    def emit(self, eng, fn, reads=(), writes=(), dma=None):
        deps = []
        for r in reads:
            if r in self.res_w:
                deps.append(self.res_w[r])
        for w in writes:
            if w in self.res_w:
                deps.append(self.res_w[w])
            for n, v in self.res_r.get(w, {}).items():
                deps.append((n, v))
        agg = {}
        for n, v in deps:
            if agg.get(n, 0) < v:
                agg[n] = v
        for t in agg.items():
            self._wait(eng, t)
        if dma is None:
            name = "eng_" + eng
            self.count[eng] += 1
            tok = (name, self.count[eng])
            self.prog[eng].append(("op", fn, name, 1))
        else:
            self._sem(dma)
            self.semval[dma] += 16
            tok = (dma, self.semval[dma])
            self.prog[eng].append(("op", fn, dma, 16))
        for w in writes:
            self.res_w[w] = tok
            self.res_r[w] = {}
        for r in reads:
            d = self.res_r.setdefault(r, {})
            if d.get(tok[0], 0) < tok[1]:
                d[tok[0]] = tok[1]
        return tok

    def emit_group(self, eng, fns, reads=(), writes=(), dma=None):
        deps = []
        for r in reads:
            if r in self.res_w:
                deps.append(self.res_w[r])
        for w in writes:
            if w in self.res_w:
                deps.append(self.res_w[w])
            for n, v in self.res_r.get(w, {}).items():
                deps.append((n, v))
        agg = {}
        for n, v in deps:
            if agg.get(n, 0) < v:
                agg[n] = v
        for t in agg.items():
            self._wait(eng, t)
        self._sem(dma)
        for fn in fns:
            self.semval[dma] += 16
            self.prog[eng].append(("op", fn, dma, 16))
        tok = (dma, self.semval[dma])
        for w in writes:
            self.res_w[w] = tok
            self.res_r[w] = {}
        for r in reads:
            d = self.res_r.setdefault(r, {})
            if d.get(tok[0], 0) < tok[1]:
                d[tok[0]] = tok[1]
        return tok

    def barrier(self):
        toks = [("eng_" + e, self.count[e]) for e in self.ENG if self.count[e] > 0]
        toks += [(n, v) for n, v in self.semval.items() if not n.startswith("eng_") and v > 0]
        for e in self.ENG:
            for t in toks:
                self._wait(e, t)

    def finish(self, final_tokens):
        agg = {}
        for n, v in final_tokens:
            if agg.get(n, 0) < v:
                agg[n] = v
        for t in agg.items():
            self._wait("sp", t)
        nc = self.nc
        sems = self.sems
        prog = self.prog

        pe_map = {}
        cum = 0
        vi = 0
        for it in prog["pe"]:
            if it[0] == "op":
                vi += 1
                if vi in self.pe_inc:
                    cum += 1
                    pe_map[vi] = cum
        pe_inc = self.pe_inc

        def replay(engname, e):
            vidx = 0
            for it in prog[engname]:
                if it[0] == "wait":
                    val = pe_map[it[2]] if it[1] == "eng_pe" else it[2]
                    e.wait_ge(sems[it[1]], val)
                elif engname == "pe" and it[2] == "eng_pe":
                    vidx += 1
                    ins = it[1](e)
                    if vidx in pe_inc:
                        ins.then_inc(sems[it[2]], it[3])
                else:
                    it[1](e).then_inc(sems[it[2]], it[3])

        with nc.Block() as block:
            @block.tensor
            def _(e):
                replay("pe", e)

            @block.scalar
            def _(e):
                replay("act", e)

            @block.vector
            def _(e):
                replay("dve", e)

            @block.gpsimd
            def _(e):
                replay("pool", e)

            @block.sync
            def _(e):
                replay("sp", e)


def build_program(nseg=NSEG, dbg=False):
    nc = bass.Bass("TRN2", target_bir_lowering=False)

    def dram(name, shape, kind="ExternalInput"):
        return nc.dram_tensor(name, shape, F32, kind=kind).ap()

    x = dram("x", [4096, 1024])
    ng = dram("norm_gain", [1024])
    w_in = dram("w_in", [1024, 3072])
    a_re = dram("ssm_a_re", [32, 64])
    a_im = dram("ssm_a_im", [32, 64])
    log_dt = dram("ssm_log_dt", [32])
    b_re = dram("ssm_b_re", [32, 64, 16])
    b_im = dram("ssm_b_im", [32, 64, 16])
    c_re = dram("ssm_c_re", [32, 16, 64])
    c_im = dram("ssm_c_im", [32, 16, 64])
    d_sk = dram("ssm_d", [32, 16])
    w_glu = dram("w_glu", [512, 512])
    b_glu = dram("b_glu", [512])
    conv_w = dram("conv_w", [3, 512])
    w_out = dram("w_out", [1024, 1024])
    fng = dram("final_norm_gain", [1024])
    out = dram("out", [4096, 1024], kind="ExternalOutput")

    xv = x.rearrange("(g j s) d -> g s j d", g=8, j=64, s=8)
    ov = out.rearrange("(g j s) d -> g s j d", g=8, j=64, s=8)

    with ExitStack() as st:
        S = Sched(nc, st)

        tstack = [st]

        def T(name, shape, dt=F32):
            return tstack[0].enter_context(nc.sbuf_tensor(name, shape, dt))

        def P(name, shape=(128, 512), dt=F32):
            return st.enter_context(nc.psum_tensor(name, list(shape), dt))

        w_in_bf = T("w_in_bf", [128, 8, 3072], BF16)
        w_out_bf = T("w_out_bf", [128, 8, 1024], BF16)
        w_glu_bf = T("w_glu_bf", [128, 4, 512], BF16)
        BbigT = T("BbigT", [128, 32, 128], BF16)
        Kb0T = T("Kb0T", [128, 32, 128], BF16)
        BbigswT = T("BbigswT", [128, 32, 128], BF16)
        CbT = T("CbT", [128, 32, 128], F32)
        RHO0 = T("RHO0", [128, 32], F32)
        COSm = T("COSm", [128, 32, 64], F32)
        SINm = T("SINm", [128, 32, 64], F32)
        ident = T("ident", [128, 128], F32)
        identb = T("identb", [128, 128], BF16)
        PswT = T("PswT", [128, 128], F32)
        smallT = T("smallT", [128, 24], F32)
        gcol = smallT[:, 0:8]
        bglu = smallT[:, 8:12]
        fg = T("fg", [128, 1024], F32)
        epsT = T("epsT", [128, 1], F32)

        PPn = 6
        PP = [P(f"pp{j}") for j in range(PPn)]
        PTB = [P(f"ptb{j}", (128, 1024), BF16) for j in range(2)]
        pp_ctr = [0]
        pt_ctr = [0]

        def next_pt():
            j = pt_ctr[0] % 2
            pt_ctr[0] += 1
            return PTB[j], ("ptb", j)

        def next_pp():
            j = pp_ctr[0] % PPn
            pp_ctr[0] += 1
            return PP[j], ("pp", j)

        st_setup = ExitStack()
        tstack[0] = st_setup
        S.emit("pool", lambda e: e.memset(ident[:], 0.0), writes=["ident"])
        S.emit("pool", lambda e: e.affine_select(out=ident[:], in_=ident[:], compare_op=ALU.not_equal, fill=1.0,
                                                 base=0, pattern=[[-1, 128]], channel_multiplier=1),
               reads=["ident"], writes=["ident"])
        S.emit("dve", lambda e: e.tensor_copy(out=identb[:], in_=ident[:]), reads=["ident"], writes=["identb"])
        offd = T("offd", [128, 128], F32)
        S.emit("pool", lambda e: e.memset(offd[:], 0.0), writes=["offd"])
        S.emit("pool", lambda e: e.affine_select(out=offd[:], in_=offd[:], compare_op=ALU.not_equal, fill=1.0,
                                                 base=64, pattern=[[-1, 128]], channel_multiplier=1),
               reads=["offd"], writes=["offd"])
        S.emit("pool", lambda e: e.affine_select(out=offd[:], in_=offd[:], compare_op=ALU.not_equal, fill=1.0,
                                                 base=-64, pattern=[[-1, 128]], channel_multiplier=1),
               reads=["offd"], writes=["offd"])
        mask = T("mask", [128, 128], F32)
        S.emit("pool", lambda e: e.memset(mask[:], 1.0), writes=["mask"])
        S.emit("pool", lambda e: e.affine_select(out=mask[:].rearrange("p (r h) -> p r h", r=8, h=16),
                                                 in_=mask[:].rearrange("p (r h) -> p r h", r=8, h=16),
                                                 compare_op=ALU.is_ge, fill=0.0,
                                                 base=15, pattern=[[16, 8], [0, 16]], channel_multiplier=-1),
               reads=["mask"], writes=["mask"])
        SG = T("SG", [128, 1], F32)
        NSG = T("NSG", [128, 1], F32)
        S.emit("pool", lambda e: e.memset(SG[0:64, :], -1.0), writes=["SGa"])
        S.emit("pool", lambda e: e.memset(SG[64:128, :], 1.0), writes=["SGb"])
        S.emit("pool", lambda e: e.memset(NSG[0:64, :], 1.0), writes=["NSGa"])
        S.emit("pool", lambda e: e.memset(NSG[64:128, :], -1.0), writes=["NSGb"])
        SGk = ["SGa", "SGb"]
        NSGk = ["NSGa", "NSGb"]
        S.emit("pool", lambda e: e.memset(epsT[:], EPS), writes=["epsT"])
        S.emit("dve", lambda e: e.tensor_scalar(out=PswT[:], in0=offd[:], scalar1=NSG[:, 0:1], scalar2=None, op0=ALU.mult),
               reads=["offd"] + NSGk, writes=["PswT"])

        small24 = T("small24", [24, 128], F32)
        S.emit("sp", lambda e: e.dma_start(out=small24[0:8, :], in_=ng.rearrange("(k p) -> k p", p=128)), writes=[("small24", 0)], dma="ld_s0")
        S.emit("sp", lambda e: e.dma_start(out=small24[8:12, :], in_=b_glu.rearrange("(k p) -> k p", p=128)), writes=[("small24", 1)], dma="ld_s1")
        S.emit("sp", lambda e: e.dma_start(out=small24[12:24, :], in_=conv_w.rearrange("k (c p) -> (k c) p", p=128)), writes=[("small24", 2)], dma="ld_s2")
        pp, ppk = next_pp()
        S.emit("pe", lambda e, pp=pp: e.transpose(out=pp[:, 0:24], in_=small24[:, :], identity=ident[0:24, 0:24]),
               reads=[("small24", 0), ("small24", 1), ("small24", 2), "ident"], writes=[ppk])
        S.emit("dve", lambda e, pp=pp: e.tensor_copy(out=smallT[:], in_=pp[:, 0:24]), reads=[ppk], writes=["gcol", "bglu", "convw"])
        convwk = ["convw"]
        S.emit("sp", lambda e: e.dma_start(out=fg[:], in_=bass.AP(fng.tensor, 0, [[0, 128], [1, 1024]])),
               writes=["fg"], dma="ld_s3")

        NE = 17

        def eidx(ev):
            return 17 if ev == 32 else ev + 8

        a32 = T("a32", [32, 2, 128], F32)
        for j, src in enumerate((a_re, a_im)):
            for h in range(2):
                S.emit("sp", lambda e, j=j, h=h, src=src: e.dma_start(out=a32[:, j, 64 * h:64 * h + 64], in_=src),
                       writes=[("a32", j, h)], dma=f"ld_a32_{j}{h}")
        ldt = T("ldt", [128, 32], F32)
        S.emit("sp", lambda e: e.dma_start(out=ldt[:], in_=bass.AP(log_dt.tensor, 0, [[0, 128], [1, 32]])),
               writes=["ldt"], dma="ld_s5")
        BA = T("BA", [128, 32, 16], F32)
        BB = T("BB", [128, 32, 16], F32)
        brv = b_re.rearrange("g p h -> p g h")
        biv = b_im.rearrange("g p h -> p g h")
        S.emit("sp", lambda e: e.dma_start(out=BA[0:64], in_=brv), writes=["BA0"], dma="ld_s6")
        S.emit("sp", lambda e: e.dma_start(out=BA[64:128], in_=biv), writes=["BA1"], dma="ld_s7")
        S.emit("sp", lambda e: e.dma_start(out=BB[0:64], in_=biv), writes=["BB0"], dma="ld_s8")
        S.emit("sp", lambda e: e.dma_start(out=BB[64:128], in_=brv), writes=["BB1"], dma="ld_s9")
        BAk = ["BA0", "BA1"]
        BBk = ["BB0", "BB1"]
        Cin1 = T("Cin1", [128, 4, 128], F32)
        Cin2 = T("Cin2", [128, 4, 128], F32)
        crv = c_re.rearrange("(a g) h p -> (g h) a p", a=4)
        civ = c_im.rearrange("(a g) h p -> (g h) a p", a=4)
        S.emit("sp", lambda e: e.dma_start(out=Cin1[:, :, 0:64], in_=crv), writes=["Cin1a"], dma="ld_s10")
        S.emit("sp", lambda e: e.dma_start(out=Cin1[:, :, 64:128], in_=civ), writes=["Cin1b"], dma="ld_s11")
        S.emit("sp", lambda e: e.dma_start(out=Cin2[:, :, 0:64], in_=civ), writes=["Cin2a"], dma="ld_s12")
        S.emit("sp", lambda e: e.dma_start(out=Cin2[:, :, 64:128], in_=crv), writes=["Cin2b"], dma="ld_s13")
        dd = T("dd", [32, 8, 16], F32)
        S.emit("sp", lambda e: e.dma_start(out=dd[:], in_=bass.AP(d_sk.tensor, 0, [[16, 32], [0, 8], [1, 16]])),
               writes=["dd"], dma="ld_s14")

        aT = T("aT", [128, 2, 32], F32)
        dcol = T("dcol", [128, 32], F32)
        pp, ppk = next_pp()
        for j in range(2):
            S.emit("pe", lambda e, j=j, pp=pp: e.transpose(out=pp[:, 32 * j:32 * j + 32], in_=a32[:, j, :], identity=ident[0:32, 0:32]),
                   reads=[("a32", j, 0), ("a32", j, 1), "ident"], writes=[ppk])
        S.emit("pe", lambda e, pp=pp: e.transpose(out=pp[:, 64:96], in_=dd[:].rearrange("p a b -> p (a b)"), identity=ident[0:32, 0:32]),
               reads=["dd", "ident"], writes=[ppk])
        S.emit("dve", lambda e, pp=pp: e.tensor_copy(out=aT[:].rearrange("p a b -> p (a b)"), in_=pp[:, 0:64]), reads=[ppk], writes=["aT"])
        S.emit("dve", lambda e, pp=pp: e.tensor_copy(out=dcol[:], in_=pp[:, 64:96]), reads=[ppk], writes=["dcol"])
        V1 = T("V1", [128, 32, 16], F32)
        V2 = T("V2", [128, 32, 16], F32)
        for (Cin, Vt, ck, vk) in ((Cin1, V1, ["Cin1a", "Cin1b"], "V1"), (Cin2, V2, ["Cin2a", "Cin2b"], "V2")):
            pp, ppk = next_pp()
            for a in range(4):
                S.emit("pe", lambda e, a=a, pp=pp, Cin=Cin: e.transpose(out=pp[:, 128 * a:128 * a + 128], in_=Cin[:, a, :], identity=ident[:]),
                       reads=ck + ["ident"], writes=[ppk])
            S.emit("dve", lambda e, pp=pp, Vt=Vt: e.tensor_copy(out=Vt[:].rearrange("p g h -> p (g h)"), in_=pp[:]), reads=[ppk], writes=[vk])
        S.emit("dve", lambda e: e.tensor_scalar(out=V1[64:128], in0=V1[64:128], scalar1=-1.0, scalar2=None, op0=ALU.mult),
               reads=["V1"], writes=["V1"])

        dt = T("dt", [128, 32], F32)
        dtr = T("dtr", [128, 32], F32)
        dti = T("dti", [128, 32], F32)
        S.emit("act", lambda e: e.activation(out=dt[:], in_=ldt[:], func=AF.Exp), reads=["ldt"], writes=["dt"])
        S.emit("dve", lambda e: e.tensor_tensor(out=dtr[:], in0=dt[:], in1=aT[:, 0, :], op=ALU.mult), reads=["dt", "aT"], writes=["dtr"])
        S.emit("dve", lambda e: e.tensor_tensor(out=dti[:], in0=dt[:], in1=aT[:, 1, :], op=ALU.mult), reads=["dt", "aT"], writes=["dti"])
        EX = T("EX", [128, NE], F32)
        for ev in list(range(-8, 9)):
            S.emit("pool", lambda e, ev=ev: e.memset(EX[:, eidx(ev):eidx(ev) + 1], float(ev)), writes=[("EX", ev)])
        EXk = [("EX", ev) for ev in list(range(-8, 9))]
        MAG = T("MAG", [128, 32, NE], F32)
        SINt = T("SINt", [128, 32, NE], F32)
        COSt = T("COSt", [128, 32, NE], F32)
        ZR = T("ZR", [128, 32, NE], F32)
        ZI = T("ZI", [128, 32, NE], F32)
        exb = EX[:].unsqueeze(1).broadcast_to([128, 32, NE])
        S.emit("dve", lambda e: e.tensor_tensor(out=MAG[:], in0=dtr[:].unsqueeze(2).broadcast_to([128, 32, NE]), in1=exb, op=ALU.mult),
               reads=["dtr"] + EXk, writes=["MAG"])
        S.emit("act", lambda e: e.activation(out=MAG[:], in_=MAG[:], func=AF.Exp), reads=["MAG"], writes=["MAG"])

        def cmul_block(eng_r, eng_i, C, Sn, cname, sname, base, n, lvl, tmps, src0=None, dst0=None, midx=None, tkeys=None, wlvl=None):
            tr1, tr2, ti1, ti2 = tmps
            src0 = base if src0 is None else src0
            dst0 = base + n if dst0 is None else dst0
            midx = base + n - 1 if midx is None else midx
            wlvl = lvl + 1 if wlvl is None else wlvl
            k1, k2, k3, k4 = tkeys if tkeys is not None else (cname + "_t1", cname + "_t2", sname + "_t1", sname + "_t2")
            csrc, ssrc = C[:, :, src0:src0 + n], Sn[:, :, src0:src0 + n]
            cm = C[:, :, midx:midx + 1].broadcast_to([128, 32, n])
            sm = Sn[:, :, midx:midx + 1].broadcast_to([128, 32, n])
            rk = [(cname, l) for l in range(lvl + 1)] + [(sname, l) for l in range(lvl + 1)]
            S.emit(eng_r, lambda e: e.tensor_tensor(out=tr1[:, :, 0:n], in0=csrc, in1=cm, op=ALU.mult), reads=rk, writes=[k1])
            S.emit(eng_r, lambda e: e.tensor_tensor(out=tr2[:, :, 0:n], in0=ssrc, in1=sm, op=ALU.mult), reads=rk, writes=[k2])
            S.emit(eng_r, lambda e: e.tensor_tensor(out=C[:, :, dst0:dst0 + n], in0=tr1[:, :, 0:n], in1=tr2[:, :, 0:n], op=ALU.subtract),
                   reads=[k1, k2], writes=[(cname, wlvl)])
            S.emit(eng_i, lambda e: e.tensor_tensor(out=ti1[:, :, 0:n], in0=csrc, in1=sm, op=ALU.mult), reads=rk, writes=[k3])
            S.emit(eng_i, lambda e: e.tensor_tensor(out=ti2[:, :, 0:n], in0=ssrc, in1=cm, op=ALU.mult), reads=rk, writes=[k4])
            S.emit(eng_i, lambda e: e.tensor_tensor(out=Sn[:, :, dst0:dst0 + n], in0=ti1[:, :, 0:n], in1=ti2[:, :, 0:n], op=ALU.add),
                   reads=[k3, k4], writes=[(sname, wlvl)])

        halfpi = T("halfpi", [128, 1], F32)
        S.emit("pool", lambda e: e.memset(halfpi[:], 0.5 * math.pi), writes=["halfpi"])
        u8 = T("u8", [128, 32], F32)
        cu = T("cu", [128, 32], F32)
        su = T("su", [128, 32], F32)
        tq_a = T("tq_a", [128, 32], F32)
        tq_b = T("tq_b", [128, 32], F32)
        S.emit("dve", lambda e: e.tensor_scalar(out=u8[:], in0=dti[:], scalar1=0.125, scalar2=None, op0=ALU.mult), reads=["dti"], writes=["u8"])
        S.emit("act", lambda e: e.activation(out=su[:], in_=u8[:], func=AF.Sin), reads=["u8"], writes=["su"])
        S.emit("act", lambda e: e.activation(out=cu[:], in_=u8[:], func=AF.Sin, scale=-1.0, bias=halfpi[:, 0:1]), reads=["u8", "halfpi"], writes=["cu"])
        for _ in range(3):
            S.emit("dve", lambda e: e.tensor_tensor(out=tq_a[:], in0=cu[:], in1=cu[:], op=ALU.mult), reads=["cu"], writes=["tq_a"])
            S.emit("dve", lambda e: e.tensor_tensor(out=tq_b[:], in0=su[:], in1=su[:], op=ALU.mult), reads=["su"], writes=["tq_b"])
            S.emit("dve", lambda e: e.scalar_tensor_tensor(out=su[:], in0=cu[:], scalar=2.0, in1=su[:], op0=ALU.mult, op1=ALU.mult), reads=["cu", "su"], writes=["su"])
            S.emit("dve", lambda e: e.tensor_tensor(out=cu[:], in0=tq_a[:], in1=tq_b[:], op=ALU.subtract), reads=["tq_a", "tq_b"], writes=["cu"])
        S.emit("pool", lambda e: e.memset(COSt[:], 1.0), writes=["COSt_init"])
        S.emit("pool", lambda e: e.memset(SINt[:], 0.0), writes=["SINt_init"])
        S.emit("dve", lambda e: e.tensor_copy(out=COSt[:, :, 9], in_=cu[:]), reads=["cu", "COSt_init"], writes=[("COSt", 0)])
        S.emit("dve", lambda e: e.tensor_copy(out=SINt[:, :, 9], in_=su[:]), reads=["su", "SINt_init"], writes=[("SINt", 0)])
        cmt = [T(f"cmt{j}", [128, 32, 4], F32) for j in range(4)]
        for lvl, n in enumerate((1, 2, 4)):
            cmul_block("dve", "dve", COSt, SINt, "COSt", "SINt", 9, n, lvl, cmt)
        posk = [("COSt", l) for l in range(4)] + [("SINt", l) for l in range(4)]
        for ev in range(1, 9):
            S.emit("dve", lambda e, ev=ev: e.tensor_copy(out=COSt[:, :, 8 - ev], in_=COSt[:, :, 8 + ev]), reads=posk, writes=[("COStn", ev)])
            S.emit("dve", lambda e, ev=ev: e.tensor_scalar(out=SINt[:, :, 8 - ev], in0=SINt[:, :, 8 + ev], scalar1=-1.0, scalar2=None, op0=ALU.mult),
                   reads=posk, writes=[("SINtn", ev)])
        allc = posk + [("COStn", ev) for ev in range(1, 9)] + ["COSt_init"]
        alls = posk + [("SINtn", ev) for ev in range(1, 9)] + ["SINt_init"]
        S.emit("dve", lambda e: e.tensor_copy(out=tq_a[:], in_=tq_a[:]), reads=allc, writes=["COSt"])
        S.emit("dve", lambda e: e.tensor_copy(out=tq_b[:], in_=tq_b[:]), reads=alls, writes=["SINt"])
        S.emit("dve", lambda e: e.tensor_tensor(out=ZR[:], in0=MAG[:], in1=COSt[:], op=ALU.mult), reads=["MAG", "COSt"], writes=["ZR"])
        S.emit("dve", lambda e: e.tensor_tensor(out=ZI[:], in0=MAG[:], in1=SINt[:], op=ALU.mult), reads=["MAG", "SINt"], writes=["ZI"])

        t = [T(f"tq{j}", [128, 32], F32) for j in range(6)]
        qr = T("qr", [128, 32], F32)
        qi = T("qi", [128, 32], F32)
        are, aim = aT[:, 0, :], aT[:, 1, :]
        zr1, zi1 = ZR[:, :, eidx(1)], ZI[:, :, eidx(1)]
        S.emit("dve", lambda e: e.tensor_tensor(out=t[0][:], in0=are, in1=are, op=ALU.mult), reads=["aT"], writes=["tq0"])
        S.emit("dve", lambda e: e.tensor_tensor(out=t[1][:], in0=aim, in1=aim, op=ALU.mult), reads=["aT"], writes=["tq1"])
        S.emit("dve", lambda e: e.tensor_tensor(out=t[0][:], in0=t[0][:], in1=t[1][:], op=ALU.add), reads=["tq0", "tq1"], writes=["tq0"])
        S.emit("dve", lambda e: e.reciprocal(out=t[0][:], in_=t[0][:]), reads=["tq0"], writes=["tq0"])
        S.emit("dve", lambda e: e.tensor_scalar(out=t[2][:], in0=zr1, scalar1=-1.0, scalar2=None, op0=ALU.add), reads=["ZR"], writes=["tq2"])
        S.emit("dve", lambda e: e.tensor_tensor(out=t[3][:], in0=t[2][:], in1=are, op=ALU.mult), reads=["tq2", "aT"], writes=["tq3"])
        S.emit("dve", lambda e: e.tensor_tensor(out=t[4][:], in0=zi1, in1=aim, op=ALU.mult), reads=["ZI", "aT"], writes=["tq4"])
        S.emit("dve", lambda e: e.tensor_tensor(out=t[3][:], in0=t[3][:], in1=t[4][:], op=ALU.add), reads=["tq3", "tq4"], writes=["tq3"])
        S.emit("dve", lambda e: e.tensor_tensor(out=qr[:], in0=t[3][:], in1=t[0][:], op=ALU.mult), reads=["tq3", "tq0"], writes=["qr"])
        S.emit("dve", lambda e: e.tensor_tensor(out=t[4][:], in0=zi1, in1=are, op=ALU.mult), reads=["ZI", "aT"], writes=["tq4"])
        S.emit("dve", lambda e: e.tensor_tensor(out=t[5][:], in0=t[2][:], in1=aim, op=ALU.mult), reads=["tq2", "aT"], writes=["tq5"])
        S.emit("dve", lambda e: e.tensor_tensor(out=t[4][:], in0=t[4][:], in1=t[5][:], op=ALU.subtract), reads=["tq4", "tq5"], writes=["tq4"])
        S.emit("dve", lambda e: e.tensor_tensor(out=qi[:], in0=t[4][:], in1=t[0][:], op=ALU.mult), reads=["tq4", "tq0"], writes=["qi"])

        W1 = T("W1", [128, 32, 16], F32)
        W2 = T("W2", [128, 32, 16], F32)
        tw = T("tw", [128, 32, 16], F32)
        qrb = qr[:].unsqueeze(2).broadcast_to([128, 32, 16])
        qib = qi[:].unsqueeze(2).broadcast_to([128, 32, 16])
        S.emit("dve", lambda e: e.tensor_tensor(out=W1[:], in0=BA[:], in1=qrb, op=ALU.mult), reads=BAk + ["qr"], writes=["W1"])
        S.emit("dve", lambda e: e.tensor_tensor(out=tw[:], in0=BB[:], in1=qib, op=ALU.mult), reads=BBk + ["qi"], writes=["tw"])
        S.emit("dve", lambda e: e.scalar_tensor_tensor(out=W1[:], in0=tw[:], scalar=SG[:, 0:1], in1=W1[:], op0=ALU.mult, op1=ALU.add),
               reads=["tw", "W1"] + SGk, writes=["W1"])
        S.emit("dve", lambda e: e.tensor_tensor(out=W2[:], in0=BB[:], in1=qrb, op=ALU.mult), reads=BBk + ["qr"], writes=["W2"])
        S.emit("dve", lambda e: e.tensor_scalar(out=W2[:], in0=W2[:], scalar1=SG[:, 0:1], scalar2=None, op0=ALU.mult), reads=["W2"] + SGk, writes=["W2"])
        S.emit("dve", lambda e: e.tensor_tensor(out=tw[:], in0=BA[:], in1=qib, op=ALU.mult), reads=BAk + ["qi"], writes=["tw"])
        S.emit("dve", lambda e: e.tensor_tensor(out=W2[:], in0=W2[:], in1=tw[:], op=ALU.subtract), reads=["W2", "tw"], writes=["W2"])

        stg = [T(f"stg{j}", [128, 1024], F32) for j in range(2)]
        sc = [0]

        def stage_cast(src_ap, ncols, dst_ap, scale_ap, dkey):
            j = sc[0] % 2
            sc[0] += 1
            half = ncols // 2
            S.emit("sp", lambda e: e.dma_start(out=stg[j][:, 0:ncols], in_=src_ap), writes=[("stg", j)], dma=f"ld_stg{j}")
            if scale_ap is None:
                S.emit("dve", lambda e: e.tensor_copy(out=dst_ap[:, 0:half], in_=stg[j][:, 0:half]), reads=[("stg", j)], writes=[(dkey, 0)])
                S.emit("dve", lambda e: e.tensor_copy(out=dst_ap[:, half:ncols], in_=stg[j][:, half:ncols]),
                       reads=[("stg", j)], writes=[(dkey, 1)])
            else:
                S.emit("dve", lambda e: e.tensor_scalar(out=dst_ap[:, 0:half], in0=stg[j][:, 0:half], scalar1=scale_ap, scalar2=None, op0=ALU.mult),
                       reads=[("stg", j), "gcol"], writes=[(dkey, 0)])
                S.emit("dve", lambda e: e.tensor_scalar(out=dst_ap[:, half:ncols], in0=stg[j][:, half:ncols], scalar1=scale_ap, scalar2=None, op0=ALU.mult),
                       reads=[("stg", j), "gcol"], writes=[(dkey, 1)])

        wq = []
        for k in range(8):
            for c3 in range(3):
                wq.append(lambda k=k, c3=c3: stage_cast(w_in[128 * k:128 * k + 128, 1024 * c3:1024 * c3 + 1024], 1024,
                                                        w_in_bf[:, k, 1024 * c3:1024 * c3 + 1024], smallT[:, k:k + 1], ("w_in", k, c3)))
        for k in range(8):
            wq.append(lambda k=k: stage_cast(w_out[128 * k:128 * k + 128, :], 1024, w_out_bf[:, k, :], None, ("w_out", k)))
        for k in range(4):
            wq.append(lambda k=k: stage_cast(w_glu[128 * k:128 * k + 128, :], 512, w_glu_bf[:, k, :], None, ("w_glu", k)))

        def wq_pop(n=1):
            for _ in range(n):
                if wq:
                    wq.pop(0)()


        Bbig = T("Bbig", [128, 32, 8, 16], F32)
        Bneg = T("Bneg", [128, 32, 8, 16], F32)
        tb = T("tb", [128, 32, 16], F32)
        CbTv = CbT[:].rearrange("p g (r h) -> p g r h", r=8, h=16)
        for s in range(8):
            for (dst, dk, ev, M1, M2, m1k, m2k, op2) in (
                    (Bbig, ("Bbig", s), 7 - s, W1, W2, "W1", "W2", ALU.add),
                    (Bneg, ("Bneg", s), -1 - s, W1, W2, "W1", "W2", ALU.add),
                    (None, ("CbT", s), s + 1, V1, V2, "V1", "V2", ALU.subtract)):
                dv = CbTv[:, :, s, :] if dst is None else dst[:, :, s, :]
                zrb = ZR[:, :, eidx(ev)].unsqueeze(2).broadcast_to([128, 32, 16])
                zib = ZI[:, :, eidx(ev)].unsqueeze(2).broadcast_to([128, 32, 16])
                S.emit("dve", lambda e, dv=dv, M1=M1, zrb=zrb: e.tensor_tensor(out=dv, in0=M1[:], in1=zrb, op=ALU.mult),
                       reads=[m1k, "ZR"], writes=[dk])
                S.emit("pool", lambda e, M2=M2, zib=zib: e.tensor_tensor(out=tb[:], in0=M2[:], in1=zib, op=ALU.mult),
                       reads=[m2k, "ZI"], writes=["tb"])
                S.emit("dve", lambda e, dv=dv, op2=op2: e.tensor_tensor(out=dv, in0=dv, in1=tb[:], op=op2),
                       reads=[dk, "tb"], writes=[dk])
                wq_pop(1)
        Bbigk = [("Bbig", s) for s in range(8)]
        Bnegk = [("Bneg", s) for s in range(8)]
        CbTk = [("CbT", s) for s in range(8)]

        S.emit("dve", lambda e: e.tensor_copy(out=RHO0[:], in_=MAG[:, :, eidx(8)]), reads=["MAG"], writes=["RHO0"])
        S.emit("dve", lambda e: e.tensor_copy(out=COSm[:, :, 0], in_=COSt[:, :, eidx(8)]), reads=["COSt"], writes=[("COSm", 0)])
        S.emit("dve", lambda e: e.tensor_copy(out=SINm[:, :, 0], in_=SINt[:, :, eidx(8)]), reads=["SINt"], writes=[("SINm", 0)])
        CbTbk = CbTk

        tkv = tw[:].rearrange("p (a b) h -> p a (b h)", a=4, b=8)
        for q4 in range(8):
            wq_pop(1)
            pp, ppk = next_pp()
            for j in range(4):
                g = 4 * q4 + j
                S.emit("pe", lambda e, g=g, j=j, pp=pp: e.transpose(out=pp[:, 128 * j:128 * j + 128],
                                                                   in_=Bbig[:, g].rearrange("p s h -> p (s h)"), identity=ident[:]),
                       reads=Bbigk + ["ident"], writes=[ppk])
            ppv = pp[:].rearrange("p (g k) -> p g k", g=4)
            S.emit("act", lambda e, q4=q4, pp=pp: e.activation(out=BbigT[:, 4 * q4:4 * q4 + 4, :].rearrange("p g k -> p (g k)"), in_=pp[:], func=AF.Copy),
                   reads=[ppk], writes=[("BbigT", q4)])
            S.emit("dve", lambda e, q4=q4, ppv=ppv: e.tensor_scalar(out=BbigswT[:, 4 * q4:4 * q4 + 4, 0:64], in0=ppv[:, :, 64:128], scalar1=-1.0, scalar2=None, op0=ALU.mult),
                   reads=[ppk], writes=[("BbigswT", q4, 0)])
            S.emit("dve", lambda e, q4=q4, ppv=ppv: e.tensor_copy(out=BbigswT[:, 4 * q4:4 * q4 + 4, 64:128], in_=ppv[:, :, 0:64]),
                   reads=[ppk], writes=[("BbigswT", q4, 1)])
            pp, ppk = next_pp()
            for j in range(4):
                g = 4 * q4 + j
                S.emit("pe", lambda e, g=g, j=j, pp=pp: e.matmul(pp[:, 128 * j:128 * j + 128], lhsT=Bneg[:, g].rearrange("p s h -> p (s h)"),
                                                                rhs=CbT[:, g, :], start=True, stop=True),
                       reads=Bnegk + CbTk, writes=[ppk])
            S.emit("dve", lambda e, pp=pp: e.tensor_tensor(out=tkv, in0=pp[:].rearrange("p (g k) -> p g k", g=4),
                                                          in1=mask[:].unsqueeze(1).broadcast_to([128, 4, 128]), op=ALU.mult),
                   reads=[ppk, "mask"], writes=["tw"])
            for j in range(4):
                g = 4 * q4 + j
                S.emit("dve", lambda e, g=g, j=j: e.scalar_tensor_tensor(out=Kb0T[:, g, :], in0=ident[:], scalar=dcol[:, g:g + 1], in1=tkv[:, j, :],
                                                                        op0=ALU.mult, op1=ALU.add),
                       reads=["ident", "dcol", "tw"], writes=[("Kb0T", g)])

        wq_pop(100)

        def wk(name, k):
            if name == "w_in":
                return [((name, k, c3), h) for c3 in range(3) for h in range(2)]
            return [((name, k), 0), ((name, k), 1)]

        S.barrier()
        st_setup.close()
        tstack[0] = st
        xin = [T(f"xin{j}", [128, 1024], F32) for j in range(2)]
        xnb = [T(f"xnb{j}", [128, 1024], BF16) for j in range(2)]
        ssq = [T(f"ssq{j}", [128, 1], F32) for j in range(2)]
        rstd = [T(f"rstd{j}", [128, 1], F32) for j in range(2)]
        xnT = T("xnT", [128, 8, 512], BF16)
        UTbuf = T("UTbuf", [128, 4096], BF16)
        UT = UTbuf[:, 0:2048].rearrange("p (g k h) -> p g k h", g=32, k=4, h=16)
        GT = UTbuf[0:64, :].rearrange("p (a k) -> p a k", a=32, k=128)
        U_st = T("U_st", [128, 32, 64], BF16)
        gy = T("gy", [128, 4, 512], BF16)
        zs = T("zs", [128, 4, 512], BF16)
        yconv = T("yconv", [128, 4, 512], BF16)
        vb0 = T("vb0", [128, 512], F32)
        vb = [vb0, vb0]
        carry = T("carry", [128, 4, 2], F32)
        csb = T("csb", [128, 512], F32)
        acc = T("acc", [128, 512], F32)
        szt = T("szt", [128, 512], F32)
        r1 = T("r1", [128, 512], F32)
        r2 = T("r2", [128, 512], F32)
        bt = T("bt", [128, 512], F32)
        xt = T("xt", [128, 512], F32)
        Xb0 = T("Xb0", [128, 8, 64], F32)
        Xb = [Xb0, Xb0]
        Xc = T("Xc", [128, 32], F32)
        Y_st0 = T("Y_st0", [128, 8, 64], BF16)
        Y_st = [Y_st0, Y_st0]
        sig = T("sig", [128, 512], F32)
        dmy = T("dmy", [128, 2], F32)
        xres = [T(f"xres{j}", [128, 1024], F32) for j in range(3)]
        ssq2 = [T(f"ssq2{j}", [128, 1], F32) for j in range(3)]
        rstd2 = [T(f"rstd2{j}", [128, 1], F32) for j in range(3)]

        def build_rot_tables():
            rtm = [t_[:].rearrange("p (g c) -> p g c", g=32) for t_ in (r1, r2, bt, xt)]
            rtk = ("r1", "r2", "bt", "xt")
            for lvl, n in enumerate((1, 2, 4, 8, 16)):
                cmul_block("dve", "pool", COSm, SINm, "COSm", "SINm", 0, n, lvl, rtm, tkeys=rtk)
            cmul_block("dve", "pool", COSm, SINm, "COSm", "SINm", 0, 16, 5, rtm, src0=0, dst0=32, midx=31, tkeys=rtk, wlvl=6)
            cmul_block("dve", "pool", COSm, SINm, "COSm", "SINm", 0, 16, 5, rtm, src0=16, dst0=48, midx=31, tkeys=rtk, wlvl=7)
            dm2 = T("dm2", [128, 2], F32)
            S.emit("dve", lambda e: e.memset(dm2[:, 0:1], 0.0), reads=[("COSm", l) for l in range(8)] + ["r1", "r2"], writes=["COSm"])
            S.emit("pool", lambda e: e.memset(dm2[:, 1:2], 0.0), reads=[("SINm", l) for l in range(8)] + ["bt", "xt"], writes=["SINm"])


        tile_ctr = [0]
        out_tokens = []

        def load_rows(dst, view, sg, k, key, sem, eng="sp"):
            fns = []
            for sl in range(2):
                fns.append(lambda e, sl=sl: e.dma_start(out=dst[64 * sl:64 * sl + 64, :], in_=view[sg, k + 4 * sl]))
            S.emit_group(eng, fns, writes=[key], dma=sem)
            return [key]

        def rms_rstd(src, srck, ssq_t, rstd_t, key, junk, junkk):
            S.emit("pool", lambda e: e.memset(ssq_t[:], 0.0), writes=[key + "_ssq"])
            S.emit("act", lambda e: e.activation(out=junk[:], in_=src[:], func=AF.Square, accum_out=ssq_t[:, 0:1]),
                   reads=srck + [key + "_ssq"], writes=[junkk, key + "_ssq"])
            S.emit("act", lambda e: e.activation(out=rstd_t[:], in_=ssq_t[:], func=AF.Sqrt, scale=1.0 / 1024.0, bias=epsT[:, 0:1]),
                   reads=[key + "_ssq", "epsT"], writes=[key + "_rstd"])
            S.emit("dve", lambda e: e.reciprocal(out=rstd_t[:], in_=rstd_t[:]), reads=[key + "_rstd"], writes=[key + "_rstd"])

        xnTk = [("xnT", k) for k in range(4)]
        UTk = [("UT", k) for k in range(4)]

        s1state = {}

        def seg_stage1(sg, part):
            if part == 1:
                js = []
                for k in range(4):
                    js.append(tile_ctr[0] % 2)
                    tile_ctr[0] += 1
                s1state[sg] = dict(js=js, xks={}, pts={})
            js, xks, pts = s1state[sg]["js"], s1state[sg]["xks"], s1state[sg]["pts"]

            def sA(k):
                j = js[k]
                xks[k] = load_rows(xin[j], xv, sg, k, f"xin{j}", f"ld_x{j}")
                rms_rstd(xin[j], xks[k], ssq[j], rstd[j], f"n1_{j}", xnb[j], f"xnb{j}")

            def sB(k):
                j = js[k]
                S.emit("dve", lambda e, j=j: e.tensor_scalar(out=xnb[j][:], in0=xin[j][:], scalar1=rstd[j][:, 0:1], scalar2=None, op0=ALU.mult),
                       reads=xks[k] + [f"n1_{j}_rstd"], writes=[f"xnb{j}"])

            def sC(k):
                j = js[k]
                pt, ptk = next_pt()
                pts[k] = (pt, ptk)
                for dc in range(8):
                    S.emit("pe", lambda e, j=j, dc=dc, pt=pt: e.transpose(out=pt[:, 128 * dc:128 * dc + 128], in_=xnb[j][:, 128 * dc:128 * dc + 128], identity=identb[:]),
                           reads=[f"xnb{j}", "identb"], writes=[ptk])

            def sD(k):
                pt, ptk = pts[k]
                S.emit("act", lambda e, k=k, pt=pt: e.activation(out=xnT[:, :, 128 * k:128 * k + 128], in_=pt[:].rearrange("p (a b) -> p a b", a=8), func=AF.Copy),
                       reads=[ptk], writes=[("xnT", k)])

            if part == 1:
                sA(0); sA(1); sB(0); sB(1)
            elif part == 2:
                sC(0); sC(1); sD(0); sA(2); sD(1); sA(3); sB(2); sB(3)
            else:
                sC(2); sC(3); sD(2); sD(3)

        def seg_main(sg):
            first = (sg % 4 == 0)

            def proj(col_chunk):
                pp, ppk = next_pp()
                for dc in range(8):
                    S.emit("pe", lambda e, dc=dc, pp=pp: e.matmul(pp[:], lhsT=w_in_bf[:, dc, 128 * col_chunk:128 * col_chunk + 128], rhs=xnT[:, dc, :],
                                                                 start=(dc == 0), stop=(dc == 7)),
                           reads=xnTk + wk("w_in", dc), writes=[ppk])
                return pp, ppk

            for k in range(4):
                pp, ppk = next_pp()
                for dc in range(8):
                    S.emit("pe", lambda e, dc=dc, pp=pp, k=k: e.matmul(pp[:], lhsT=xnT[:, dc, 128 * k:128 * k + 128], rhs=w_in_bf[:, dc, 0:512],
                                                                      start=(dc == 0), stop=(dc == 7)),
                           reads=[("xnT", k)] + wk("w_in", dc), writes=[ppk])
                if k % 2 == 0:
                    S.emit("act", lambda e, pp=pp, k=k: e.activation(out=UT[:, :, k, :], in_=pp[:].rearrange("p (g h) -> p g h", g=32), func=AF.Copy),
                           reads=[ppk], writes=[("UT", k)])
                else:
                    S.emit("dve", lambda e, pp=pp, k=k: e.tensor_copy(out=UT[:, :, k, :], in_=pp[:].rearrange("p (g h) -> p g h", g=32)),
                           reads=[ppk], writes=[("UT", k)])
            pz0, pz0k = proj(4)
            S.emit("act", lambda e, pz0=pz0: e.activation(out=zs[:, 0, :], in_=pz0[:], func=AF.Silu), reads=[pz0k], writes=[("zs", 0)])
            for hb in range(2):
                pt, ptk = next_pt()
                for gg in range(16):
                    g = 16 * hb + gg
                    for sl in range(2):
                        S.emit("pe", lambda e, g=g, gg=gg, pt=pt, sl=sl: e.transpose(out=pt[64 * sl:64 * sl + 64, 64 * gg:64 * gg + 64],
                                                                                   in_=UT[64 * sl:64 * sl + 64, g, :, :].rearrange("p k h -> p (k h)"),
                                                                                   identity=identb[64 * sl:64 * sl + 64, 64 * sl:64 * sl + 64]),
                               reads=UTk + ["identb"], writes=[ptk])
                if hb == 0:
                    S.emit("act", lambda e, hb=hb, pt=pt: e.activation(out=U_st[:, 16 * hb:16 * hb + 16, :].rearrange("p g c -> p (g c)"), in_=pt[:], func=AF.Copy),
                           reads=[ptk], writes=[("U_st", 2 * hb), ("U_st", 2 * hb + 1)])
                else:
                    S.emit("dve", lambda e, hb=hb, pt=pt: e.tensor_copy(out=U_st[:, 16 * hb:16 * hb + 16, :].rearrange("p g c -> p (g c)"), in_=pt[:]),
                           reads=[ptk], writes=[("U_st", 2 * hb), ("U_st", 2 * hb + 1)])

            if sg == 0:
                S.emit("pool", lambda e: e.memset(dmy[:], 0.0), writes=["dmy0", "dmy1"])
            if first:
                S.emit("pool", lambda e: e.memset(Xc[:], 0.0), writes=[("Xc", b) for b in range(4)])
                S.emit("pool", lambda e: e.memset(carry[:], 0.0), writes=[("carry", i) for i in range(4)])

            stt = {}

            def fma(dst, src, w, rk):
                S.emit("dve", lambda e: e.scalar_tensor_tensor(out=dst, in0=src, scalar=w, in1=dst, op0=ALU.mult, op1=ALU.add),
                       reads=rk + convwk + ["acc"], writes=["acc"])

            def v2(ap, off, n):
                return bass.AP(ap.tensor, ap.offset + off, [list(ap.ap[0]), [64, 2], [1, n]])

            def stA(b):
                ps, psk = next_pp()
                psw, pswk = next_pp()
                for gg in range(8):
                    g = 8 * b + gg
                    S.emit("pe", lambda e, ps=ps, g=g, gg=gg: e.matmul(ps[:, 64 * gg:64 * gg + 64], lhsT=BbigT[:, g, :], rhs=U_st[:, g, :], start=True, stop=True),
                           reads=[("U_st", b), ("BbigT", g // 4)], writes=[psk])
                for gg in range(8):
                    g = 8 * b + gg
                    S.emit("pe", lambda e, psw=psw, g=g, gg=gg: e.matmul(psw[:, 64 * gg:64 * gg + 64], lhsT=BbigswT[:, g, :], rhs=U_st[:, g, :], start=True, stop=True),
                           reads=[("U_st", b), ("BbigswT", g // 4, 0), ("BbigswT", g // 4, 1)], writes=[pswk])
                stt[b] = dict(ps=ps, psk=psk, psw=psw, pswk=pswk)

            def stZ(b):
                pz, pzk = proj(4 + b)
                S.emit("act", lambda e, b=b, pz=pz: e.activation(out=zs[:, b, :], in_=pz[:], func=AF.Silu), reads=[pzk], writes=[("zs", b)])

            def stR(b):
                d = stt[b]
                gs = slice(8 * b, 8 * b + 8)
                cosb = COSm[:, gs, :].rearrange("p g c -> p (g c)")
                sinb = SINm[:, gs, :].rearrange("p g c -> p (g c)")
                d["cosb"], d["sinb"], d["gs"] = cosb, sinb, gs
                ps, psw = d["ps"], d["psw"]
                S.emit("dve", lambda e, ps=ps, cosb=cosb: e.tensor_tensor(out=r1[:], in0=ps[:], in1=cosb, op=ALU.mult), reads=[d["psk"], "COSm"], writes=["r1"])
                S.emit("dve", lambda e, psw=psw, sinb=sinb: e.tensor_tensor(out=r2[:], in0=psw[:], in1=sinb, op=ALU.mult), reads=[d["pswk"], "SINm"], writes=["r2"])
                S.emit("pool", lambda e: e.tensor_tensor(out=bt[:], in0=r1[:], in1=r2[:], op=ALU.subtract), reads=["r1", "r2"], writes=["bt"])
                for gg in range(8):
                    g = 8 * b + gg
                    S.emit("dve", lambda e, g=g, gg=gg: e.tensor_tensor_scan(out=xt[:, 64 * gg:64 * gg + 64], data0=RHO0[:, g:g + 1].broadcast_to([128, 64]),
                                                                           data1=bt[:, 64 * gg:64 * gg + 64], initial=Xc[:, g:g + 1], op0=ALU.mult, op1=ALU.add),
                           reads=["bt", "RHO0", ("Xc", b)], writes=[("xt", gg)])

            def stC1(b):
                i = b
                ph, phk = proj(8 + i)
                pc, pck = proj(16 + i)
                S.emit("act", lambda e, pc=pc: e.activation(out=csb[:], in_=pc[:], func=AF.Copy), reads=[pck], writes=["csb"])
                v = vb[0]
                vk_ = ("vb", 0)
                S.emit("dve", lambda e, v=v, ph=ph: e.tensor_tensor(out=v[:], in0=ph[:], in1=csb[:], op=ALU.mult), reads=[phk, "csb"], writes=[vk_])
                w2 = smallT[:, 20 + i:21 + i]
                S.emit("act", lambda e, v=v, w2=w2: e.activation(out=acc[:], in_=v[:], func=AF.Copy, scale=w2), reads=[vk_] + convwk, writes=["acc"])

            def stX(b):
                d = stt[b]
                gs, cosb, sinb = d["gs"], d["cosb"], d["sinb"]
                xtk = [("xt", gg) for gg in range(8)]
                px, pxk = next_pp()
                S.emit("pe", lambda e, px=px: e.matmul(px[:], lhsT=PswT[:], rhs=xt[:], start=True, stop=True), reads=xtk + ["PswT"], writes=[pxk])
                S.emit("pool", lambda e, cosb=cosb: e.tensor_tensor(out=r1[:], in0=xt[:], in1=cosb, op=ALU.mult), reads=xtk + ["COSm"], writes=["r1"])
                S.emit("dve", lambda e, px=px, sinb=sinb: e.tensor_tensor(out=r2[:], in0=px[:], in1=sinb, op=ALU.mult), reads=[pxk, "SINm"], writes=["r2"])
                xb = Xb[b % 2]
                xbk = ("Xb", 0)
                d["xb"], d["xbk"] = xb, xbk
                r1v = r1[:].rearrange("p (g c) -> p g c", g=8)
                r2v = r2[:].rearrange("p (g c) -> p g c", g=8)
                S.emit("dve", lambda e, xb=xb, r1v=r1v, r2v=r2v: e.tensor_tensor(out=xb[:, :, 1:64], in0=r1v[:, :, 0:63], in1=r2v[:, :, 0:63], op=ALU.add),
                       reads=["r1", "r2"], writes=[xbk])
                S.emit("dve", lambda e, xb=xb, gs=gs: e.tensor_copy(out=xb[:, :, 0], in_=Xc[:, gs]), reads=[("Xc", b), xbk], writes=[xbk])
                S.emit("dve", lambda e, gs=gs, r1v=r1v, r2v=r2v: e.tensor_tensor(out=Xc[:, gs], in0=r1v[:, :, 63], in1=r2v[:, :, 63], op=ALU.add),
                       reads=["r1", "r2"], writes=[("Xc", b)])

            def stC2(b):
                i = b
                v = vb[0]
                vk_, vpk = ("vb", 0), ("carry", i)
                w0, w1 = smallT[:, 12 + i:13 + i], smallT[:, 16 + i:17 + i]
                def v3(ap, off, n):
                    return bass.AP(ap.tensor, ap.offset + off, [list(ap.ap[0]), [128, 2], [1, n]])

                fma(acc[:, 128:512], v[:, 0:384], w1, [vk_])
                fma(acc[:, 64:128], v[:, 384:448], w1, [vk_])
                fma(acc[:, 1:64], v[:, 448:511], w1, [vk_])
                fma(acc[:, 0:1], carry[:, i, 1:2], w1, [vpk])
                fma(acc[:, 256:512], v[:, 0:256], w0, [vk_])
                fma(v3(acc[:], 64, 64), v3(v[:], 256, 64), w0, [vk_])
                fma(v3(acc[:], 1, 63), v3(v[:], 320, 63), w0, [vk_])
                fma(bass.AP(acc[:].tensor, acc[:].offset, [list(acc[:].ap[0]), [128, 2]]), carry[:, i, :], w0, [vpk])
                S.emit("pool", lambda e, v=v, i=i: e.tensor_copy(out=carry[:, i, :], in_=bass.AP(v[:].tensor, v[:].offset + 383, [list(v[:].ap[0]), [128, 2]])),
                       reads=[vk_], writes=[vpk])

            def stC3(b):
                i = b
                pb, pbk = proj(12 + i)
                pzc, pzck = proj(20 + i)
                S.emit("act", lambda e, pzc=pzc: e.activation(out=szt[:], in_=pzc[:], func=AF.Silu), reads=[pzck], writes=["szt"])
                S.emit("dve", lambda e, pb=pb: e.tensor_tensor(out=acc[:], in0=pb[:], in1=acc[:], op=ALU.mult), reads=[pbk, "acc"], writes=["acc"])
                S.emit("pool", lambda e, i=i: e.tensor_tensor(out=yconv[:, i, :], in0=acc[:], in1=szt[:], op=ALU.mult), reads=["acc", "szt"], writes=[("yconv", i)])

            def stY(b):
                d = stt[b]
                xb, xbk = d["xb"], d["xbk"]
                py, pyk = next_pp()
                for gg in range(8):
                    g = 8 * b + gg
                    S.emit("pe", lambda e, py=py, g=g, gg=gg: e.matmul(py[:, 64 * gg:64 * gg + 64], lhsT=Kb0T[:, g, :], rhs=U_st[:, g, :], start=True, stop=False),
                           reads=[("U_st", b), ("Kb0T", g)], writes=[pyk])
                    S.emit("pe", lambda e, py=py, g=g, gg=gg, xb=xb: e.matmul(py[:, 64 * gg:64 * gg + 64], lhsT=CbT[:, g, :], rhs=xb[:, gg, :], start=False, stop=True),
                           reads=CbTbk + [xbk], writes=[pyk])
                yst = Y_st[0]
                ystk = ("Y_st", 0)
                S.emit("act", lambda e, py=py, yst=yst: e.activation(out=yst[:].rearrange("p g c -> p (g c)"), in_=py[:], func=AF.Copy), reads=[pyk], writes=[ystk])
                S.emit("act", lambda e: e.activation(out=dmy[:, 1:2], in_=dmy[:, 0:1], func=AF.Gelu_apprx_tanh), reads=["dmy0"], writes=["dmy1"])

            GTv = GT.rearrange("p (r b2) k -> p r b2 k", r=8, b2=4)

            def stT1(b):
                yst = Y_st[0]
                ystk = ("Y_st", 0)
                pt, ptk = next_pt()
                for gg in range(8):
                    S.emit("pe", lambda e, gg=gg, pt=pt, yst=yst: e.transpose(out=pt[0:64, 128 * gg:128 * gg + 128], in_=yst[:, gg, :], identity=identb[:]),
                           reads=[ystk, "identb"], writes=[ptk])
                S.emit("act", lambda e, pt=pt, b=b: e.activation(out=GTv[:, :, b, :].rearrange("p r (g h) -> p r g h", g=8),
                                                               in_=pt[0:64, :].rearrange("p (g r h) -> p r g h", g=8, r=8, h=16), func=AF.Gelu_apprx_tanh),
                       reads=[ptk], writes=[("GT", b)] + UTk)

            def stT2(b):
                pt2, pt2k = next_pt()
                for r in range(8):
                    co = (r % 4) * 128 + (r // 4) * 64
                    S.emit("pe", lambda e, r=r, b=b, pt2=pt2, co=co: e.transpose(out=pt2[:, co:co + 64], in_=GTv[:, r, b, :], identity=identb[0:64, 0:64]),
                           reads=[("GT", b), "identb"] + UTk, writes=[pt2k])
                S.emit("dve", lambda e, pt2=pt2, b=b: e.tensor_copy(out=gy[:, b, :], in_=pt2[:, 0:512]), reads=[pt2k], writes=[("gy", b)])

            if sg == 0:
                build_rot_tables()
            stA(0)
            stR(0)
            for b in range(4):
                stC1(b)
                stX(b)
                stC2(b)
                stC3(b)
                stY(b)
                if b < 3:
                    stA(b + 1)
                stT1(b)
                if b < 3:
                    stZ(b + 1)
                    stR(b + 1)
                stT2(b)

        def seg_tail(sg, part):
            if part == 2:
                return seg_tail_out(sg, (0, 1))
            if part == 3:
                return seg_tail_out(sg, (2, 3))
            gyk = [("gy", b) for b in range(4)]
            if dbg and sg == dbg - 1:
                items = [("dbg_U", U_st, BF16, [128, 32, 64], [("U_st", b) for b in range(4)]),
                         ("dbg_gy", gy, BF16, [128, 4, 512], gyk),
                         ("dbg_yc", yconv, BF16, [128, 4, 512], [("yconv", b) for b in range(4)]),
                         ("dbg_COS", COSm, F32, [128, 32, 64], ["COSm"]),
                         ("dbg_SIN", SINm, F32, [128, 32, 64], ["SINm"]),
                         ("dbg_RHO", RHO0, F32, [128, 32], ["RHO0"]),
                         ("dbg_Xc", Xc, F32, [128, 32], [("Xc", b) for b in range(4)]),
                         ("dbg_BbigT", BbigT, BF16, [128, 32, 128], [("BbigT", q) for q in range(8)]),
                         ("dbg_BbigswT", BbigswT, BF16, [128, 32, 128], [("BbigswT", q, h) for q in range(8) for h in range(2)]),
                         ("dbg_Kb0T", Kb0T, BF16, [128, 32, 128], [("Kb0T", g) for g in range(32)]),
                         ("dbg_CbT", CbT, F32, [128, 32, 128], CbTk)]
                for (nm, tl, dt_, shp, rk) in items:
                    dap = nc.dram_tensor(nm, shp, dt_, kind="ExternalOutput").ap()
                    out_tokens.append(S.emit("sp", lambda e, dap=dap, tl=tl: e.dma_start(out=dap, in_=tl[:]), reads=rk, dma="st_dbg_" + nm))
            for eo in range(4):
                pp, ppk = next_pp()
                for ch in range(4):
                    S.emit("pe", lambda e, pp=pp, ch=ch, eo=eo: e.matmul(pp[:], lhsT=w_glu_bf[:, ch, 128 * eo:128 * eo + 128], rhs=gy[:, ch, :],
                                                                        start=(ch == 0), stop=(ch == 3)),
                           reads=gyk + wk("w_glu", ch), writes=[ppk])
                S.emit("act", lambda e, pp=pp, eo=eo: e.activation(out=sig[:], in_=pp[:], func=AF.Sigmoid, bias=smallT[:, 8 + eo:9 + eo]),
                       reads=[ppk, "bglu"], writes=["sig"])
                S.emit("dve", lambda e, eo=eo: e.tensor_tensor(out=sig[:], in0=gy[:, eo, :], in1=sig[:], op=ALU.mult), reads=[("gy", eo), "sig"], writes=["sig"])
                S.emit("dve", lambda e, eo=eo: e.tensor_tensor(out=zs[:, eo, :], in0=sig[:], in1=zs[:, eo, :], op=ALU.mult),
                       reads=["sig", ("zs", eo)], writes=[("zs", eo)])

        def seg_tail_out(sg, tiles):
            ys = zs
            junk2 = sig[:].bitcast(BF16)
            info = {}
            for k in tiles:
                j = (sg * 4 + k) % 3
                info[k] = dict(j=j, xrk=load_rows(xres[j], xv, sg, k, f"xres{j}", f"ld_r{j}", eng="pool"))
            for k in tiles:
                info[k]["pos"] = [next_pp() for dh in range(2)]
            for chs in ((4, 5, 6, 7), (0, 1, 2, 3)):
                for k in tiles:
                    for dh in range(2):
                        pp, ppk = info[k]["pos"][dh]
                        for ch in chs:
                            lhsT = ys[:, ch, 128 * k:128 * k + 128] if ch < 4 else yconv[:, ch - 4, 128 * k:128 * k + 128]
                            rk = [("zs", ch)] if ch < 4 else [("yconv", ch - 4)]
                            S.emit("pe", lambda e, pp=pp, lhsT=lhsT, ch=ch, dh=dh: e.matmul(pp[:], lhsT=lhsT, rhs=w_out_bf[:, ch, 512 * dh:512 * dh + 512],
                                                                                           start=(ch == 4), stop=(ch == 3)),
                                   reads=rk + wk("w_out", ch), writes=[ppk])
            for k in tiles:
                j, xrk = info[k]["j"], info[k]["xrk"]
                for dh in range(2):
                    pp, ppk = info[k]["pos"][dh]
                    S.emit("dve", lambda e, pp=pp, dh=dh, j=j: e.tensor_tensor(out=xres[j][:, 512 * dh:512 * dh + 512], in0=pp[:],
                                                                              in1=xres[j][:, 512 * dh:512 * dh + 512], op=ALU.add),
                           reads=[ppk] + xrk, writes=xrk)
            for k in tiles:
                j, xrk = info[k]["j"], info[k]["xrk"]
                key = f"n2_{j}"
                S.emit("pool", lambda e, j=j: e.memset(ssq2[j][:], 0.0), writes=[key + "_ssq"])
                S.emit("act", lambda e, j=j: e.activation(out=junk2, in_=xres[j][:], func=AF.Square, accum_out=ssq2[j][:, 0:1]),
                       reads=xrk + [key + "_ssq"], writes=["sig", key + "_ssq"])
                S.emit("act", lambda e, j=j: e.activation(out=rstd2[j][:], in_=ssq2[j][:], func=AF.Sqrt, scale=1.0 / 1024.0, bias=epsT[:, 0:1]),
                       reads=[key + "_ssq", "epsT"], writes=[key + "_rstd"])
            for k in tiles:
                j, xrk = info[k]["j"], info[k]["xrk"]
                key = f"n2_{j}"
                S.emit("dve", lambda e, j=j: e.reciprocal(out=rstd2[j][:], in_=rstd2[j][:]), reads=[key + "_rstd"], writes=[key + "_rstd"])
                S.emit("dve", lambda e, j=j: e.scalar_tensor_tensor(out=xres[j][:], in0=xres[j][:], scalar=rstd2[j][:, 0:1], in1=fg[:],
                                                                   op0=ALU.mult, op1=ALU.mult),
                       reads=xrk + [key + "_rstd", "fg"], writes=xrk)
                fns = []
                for sl in range(2):
                    fns.append(lambda e, sl=sl, k=k, sg=sg, j=j: e.dma_start(out=ov[sg, k + 4 * sl], in_=xres[j][64 * sl:64 * sl + 64, :]))
                out_tokens.append(S.emit_group("sp", fns, reads=xrk, dma=f"st_o{j}"))

        seg_stage1(0, 1)
        seg_stage1(0, 2)
        seg_stage1(0, 3)
        for sg in range(nseg):
            nxt = sg + 1 < nseg
            seg_main(sg)
            if nxt:
                seg_stage1(sg + 1, 1)
            seg_tail(sg, 1)
            if nxt:
                seg_stage1(sg + 1, 2)
            seg_tail(sg, 2)
            if nxt:
                seg_stage1(sg + 1, 3)
            seg_tail(sg, 3)
        S.finish(out_tokens)
    return nc


_CACHE = {}


def _get_nc():
    if "nc" not in _CACHE:
        _CACHE["nc"] = build_program()
    return _CACHE["nc"]


def _in_maps(inputs):
    f = lambda a: np.ascontiguousarray(np.asarray(a, dtype=np.float32))
    x = f(inputs["x"])
    shared = {
        "norm_gain": f(inputs["norm_gain"]).reshape(1024),
        "w_in": f(inputs["w_in"]).reshape(1024, 3072),
        "ssm_a_re": f(inputs["ssm_a_re"]).reshape(32, 64),
        "ssm_a_im": f(inputs["ssm_a_im"]).reshape(32, 64),
        "ssm_log_dt": f(inputs["ssm_log_dt"]).reshape(32),
        "ssm_b_re": f(inputs["ssm_b_re"]).reshape(32, 64, 16),
        "ssm_b_im": f(inputs["ssm_b_im"]).reshape(32, 64, 16),
        "ssm_c_re": f(inputs["ssm_c_re"]).reshape(32, 16, 64),
        "ssm_c_im": f(inputs["ssm_c_im"]).reshape(32, 16, 64),
        "ssm_d": f(inputs["ssm_d"]).reshape(32, 16),
        "w_glu": f(inputs["w_glu"]).reshape(512, 512),
        "b_glu": f(inputs["b_glu"]).reshape(512),
        "conv_w": f(inputs["conv_w"]).reshape(3, 512),
        "w_out": f(inputs["w_out"]).reshape(1024, 1024),
        "final_norm_gain": f(inputs["final_norm_gain"]).reshape(1024),
    }
    maps = []
    for r in range(NCORES):
        m = dict(shared)
        m["x"] = np.ascontiguousarray(x[2 * r:2 * r + 2].reshape(4096, 1024))
        maps.append(m)
    return maps


def kernel(**inputs):
    nc = _get_nc()
    res = run_bass_kernel_spmd(nc, _in_maps(inputs), core_ids=list(range(NCORES)))
    outs = [np.asarray(r["out"], dtype=np.float32).reshape(2, 2048, 1024) for r in res.results]
    return np.concatenate(outs, axis=0)
```
